# Optimizing a Trainium2 kernel written in Bass

```python
import jax, jax.numpy as jnp
from jax import lax
import numpy as np

D_MODEL = 2048
BATCH = 4
SEQ = 2048
DEPTH = 1
DEC_BATCH = 128
DEC_SEQ = 4
PAST_LEN = 16384
PAGE_SIZE = 128

CHUNK = 128
HEAD_DIM_A = 128
D_A = D_MODEL
H_A = D_A // HEAD_DIM_A
D_B = D_MODEL
K_CONV = 31
D_FF = 5632
EPS = 1e-6
SPLITS = (D_A, 2 * D_A, 2 * D_A + D_B, 2 * D_A + 2 * D_B, 2 * D_A + 2 * D_B + D_MODEL)
D_IN = 2 * D_A + 2 * D_B + 2 * D_MODEL

kernel_name = "hybrid_gmlp_conformer_conv_decode_step"


def rms_norm(x, g):
    xf = x.astype(jnp.float32)
    y = xf * lax.rsqrt(jnp.mean(xf * xf, axis=-1, keepdims=True) + EPS)
    return (y * g.astype(jnp.float32)).astype(x.dtype)


def layer_norm(x, g, b):
    xf = x.astype(jnp.float32)
    mu = jnp.mean(xf, axis=-1, keepdims=True)
    var = jnp.mean(jnp.square(xf - mu), axis=-1, keepdims=True)
    y = (xf - mu) * lax.rsqrt(var + EPS)
    return (y * g.astype(jnp.float32) + b.astype(jnp.float32)).astype(x.dtype)


def swiglu_ffn(x, w_gu, w_down):
    gate, up = jnp.split(x @ w_gu, 2, axis=-1)
    return (jax.nn.silu(gate) * up) @ w_down


def chunk_spatial_gating(u, v, w_s, b_s):
    n, t, _ = v.shape
    n_chunks = -(-t // CHUNK)
    pad = n_chunks * CHUNK - t
    vp = jnp.pad(v, ((0, 0), (0, pad), (0, 0)))
    vc = vp.reshape(n, n_chunks, CHUNK, H_A, HEAD_DIM_A)
    mask = jnp.tril(jnp.ones((CHUNK, CHUNK), dtype=w_s.dtype))
    w = w_s * mask[None]
    s = jnp.einsum('hij,ncjhd->ncihd', w, vc) + b_s.T[None, None, :, :, None]
    s = s.reshape(n, n_chunks * CHUNK, D_A)[:, :t]
    return u * s


def token_mixer(h, conv_prev, w_in, w_s, b_s, ln_v_g, ln_v_b, w_pa,
                w_dw, b_dw, ln_c_g, ln_c_b, w_pb, w_out):
    t = h.shape[1]
    p = h @ w_in
    ua, va, glu_a, glu_b, gate_a, gate_b = jnp.split(p, SPLITS, axis=-1)
    u = jax.nn.gelu(ua)
    v = layer_norm(jax.nn.gelu(va), ln_v_g, ln_v_b)
    y_a = chunk_spatial_gating(u, v, w_s, b_s) @ w_pa
    last_len = t - CHUNK * ((t - 1) // CHUNK)
    v_state = v[:, t - last_len:]
    g = glu_a * jax.nn.sigmoid(glu_b)
    gpad = jnp.concatenate([conv_prev.astype(g.dtype), g], axis=1)
    c = lax.conv_general_dilated(gpad, w_dw[:, None, :].astype(g.dtype), (1,), 'VALID',
                                 dimension_numbers=('NWC', 'WIO', 'NWC'),
                                 feature_group_count=D_B) + b_dw
    c = jax.nn.silu(layer_norm(c, ln_c_g, ln_c_b))
    y_b = c @ w_pb
    conv_state = gpad[:, -(K_CONV - 1):]
    m = jax.nn.sigmoid(gate_a) * y_a + jax.nn.sigmoid(gate_b) * y_b
    return m @ w_out, v_state, conv_state


def decoder_layer(x, conv_prev, ffn1_norm, ffn1_w_gu, ffn1_w_down, mix_norm, w_in, w_s, b_s,
                  ln_v_g, ln_v_b, w_pa, w_dw, b_dw, ln_c_g, ln_c_b, w_pb, w_out,
                  ffn2_norm, ffn2_w_gu, ffn2_w_down):
    x = x + 0.5 * swiglu_ffn(rms_norm(x, ffn1_norm), ffn1_w_gu, ffn1_w_down)
    mix, v_state, conv_state = token_mixer(rms_norm(x, mix_norm), conv_prev, w_in, w_s, b_s,
                                           ln_v_g, ln_v_b, w_pa, w_dw, b_dw, ln_c_g, ln_c_b,
                                           w_pb, w_out)
    x = x + mix
    x = x + 0.5 * swiglu_ffn(rms_norm(x, ffn2_norm), ffn2_w_gu, ffn2_w_down)
    return x, v_state, conv_state


def setup_inputs(seed: int = 0) -> dict:
    key = jax.random.key(seed)
    ks = jax.random.split(key, 24)
    f32 = jnp.float32

    def nrm(k, shape, scale):
        return jax.random.normal(k, shape, f32) * scale

    def gain(k, shape):
        return 1.0 + 0.02 * jax.random.normal(k, shape, f32)

    L = DEPTH
    return {
        "x_prompt": nrm(ks[0], (BATCH, SEQ, D_MODEL), 1.0),
        "x_sample": nrm(ks[1], (DEC_BATCH, DEC_SEQ, D_MODEL), 1.0),
        "cache_conv": nrm(ks[2], (L, DEC_BATCH, K_CONV - 1, D_B), 0.5),
        "ffn1_norm": gain(ks[3], (L, D_MODEL)),
        "ffn1_w_gu": nrm(ks[4], (L, D_MODEL, 2 * D_FF), D_MODEL ** -0.5),
        "ffn1_w_down": nrm(ks[5], (L, D_FF, D_MODEL), D_FF ** -0.5),
        "mix_norm": gain(ks[6], (L, D_MODEL)),
        "w_in": nrm(ks[7], (L, D_MODEL, D_IN), D_MODEL ** -0.5),
        "w_s": nrm(ks[8], (L, H_A, CHUNK, CHUNK), CHUNK ** -0.5),
        "b_s": gain(ks[9], (L, H_A, CHUNK)),
        "ln_v_g": gain(ks[10], (L, D_A)),
        "ln_v_b": nrm(ks[11], (L, D_A), 0.02),
        "w_pa": nrm(ks[12], (L, D_A, D_MODEL), D_A ** -0.5),
        "w_dw": nrm(ks[13], (L, K_CONV, D_B), K_CONV ** -0.5),
        "b_dw": nrm(ks[14], (L, D_B), 0.02),
        "ln_c_g": gain(ks[15], (L, D_B)),
        "ln_c_b": nrm(ks[16], (L, D_B), 0.02),
        "w_pb": nrm(ks[17], (L, D_B, D_MODEL), D_B ** -0.5),
        "w_out": nrm(ks[18], (L, D_MODEL, D_MODEL), D_MODEL ** -0.5),
        "ffn2_norm": gain(ks[19], (L, D_MODEL)),
        "ffn2_w_gu": nrm(ks[20], (L, D_MODEL, 2 * D_FF), D_MODEL ** -0.5),
        "ffn2_w_down": nrm(ks[21], (L, D_FF, D_MODEL), D_FF ** -0.5),
        "final_norm": gain(ks[22], (D_MODEL,)),
    }


def reference(x_prompt, x_sample, cache_conv, ffn1_norm, ffn1_w_gu, ffn1_w_down, mix_norm,
              w_in, w_s, b_s, ln_v_g, ln_v_b, w_pa, w_dw, b_dw, ln_c_g, ln_c_b, w_pb, w_out,
              ffn2_norm, ffn2_w_gu, ffn2_w_down, final_norm):
    xp, xs = x_prompt, x_sample
    vp_list, vs_list, cp_list, cs_list = [], [], [], []
    for l in range(DEPTH):
        params = (ffn1_norm[l], ffn1_w_gu[l], ffn1_w_down[l], mix_norm[l], w_in[l], w_s[l], b_s[l],
                  ln_v_g[l], ln_v_b[l], w_pa[l], w_dw[l], b_dw[l], ln_c_g[l], ln_c_b[l], w_pb[l],
                  w_out[l], ffn2_norm[l], ffn2_w_gu[l], ffn2_w_down[l])
        zero_conv = jnp.zeros((xp.shape[0], K_CONV - 1, D_B), dtype=xp.dtype)
        xp, v_p, c_p = decoder_layer(xp, zero_conv, *params)
        xs, v_s, c_s = decoder_layer(xs, cache_conv[l], *params)
        vp_list.append(v_p)
        vs_list.append(v_s)
        cp_list.append(c_p)
        cs_list.append(c_s)
    y_prompt = rms_norm(xp, final_norm)
    y_sample = rms_norm(xs, final_norm)
    chunk_v_prompt = jnp.stack(vp_list)
    chunk_v_sample = jnp.stack(vs_list)
    conv_state_prompt = jnp.stack(cp_list)
    conv_state_sample = jnp.stack(cs_list)
    return (y_prompt, y_sample, chunk_v_prompt, chunk_v_sample, conv_state_prompt, conv_state_sample)
```

```python
import numpy as np
import concourse.bass as bass
import concourse.mybir as mybir
from concourse.bass_utils import run_bass_kernel_spmd

F32 = mybir.dt.float32
BF16 = mybir.dt.bfloat16
AF = mybir.ActivationFunctionType
ALU = mybir.AluOpType

D = 2048
KC = 16
DFF = 5632
NT = 1120
NPV = 9 * 16 + 31 * 16
PV_F1, PV_MIX, PV_F2, PV_FIN, PV_BDW, PV_LCG, PV_LCB, PV_LVG, PV_LVB = range(9)
PV_WDW = 9 * 16
EPS = 1e-6
ENGS = ["pe", "act", "dve", "pool", "sp"]
NSLOT = 4
SLOT_B = 8192

OFF_X = 0
OFF_PV = 71680
OFF_IDF = OFF_PV + 2560
OFF_ONES = OFF_IDF + 512
OFF_EPS = OFF_ONES + 256
OFF_GTAIL = OFF_EPS + 16
OFF_WST = OFF_GTAIL + 1920
OFF_MS = OFF_WST + 4096
OFF_SMALL = OFF_MS + 2048
PL0 = 83456
assert OFF_SMALL + 256 <= PL0
OFF_RSTD = PL0
OFF_SLOTS = OFF_RSTD + 4608
OFF_TMPX = OFF_SLOTS + NSLOT * SLOT_B
REG0 = OFF_TMPX + 12288
OFF_H = REG0
OFF_HID = REG0 + 35840
OFF_HM = REG0
OFF_V = REG0 + 19456
OFF_US = OFF_V + 20480
OFF_CN = OFF_US + 18432
ARENA_B = REG0 + 78848


class Plan:
    def __init__(self):
        self.q = {e: [] for e in ENGS}
        self.n = {e: 0 for e in ENGS}
        self.waited = {e: {} for e in ENGS}
        self.dcnt = {}
        self.needed = {e: set() for e in ENGS}

    def wait(self, eng, ev):
        if ev is None:
            return
        k, v = ev
        if self.waited[eng].get(k, 0) >= v:
            return
        self.waited[eng][k] = v
        if k in self.needed:
            self.needed[k].add(v)
        self.q[eng].append(("wait", k, v))

    def op(self, eng, fn, deps=()):
        for d in deps:
            self.wait(eng, d)
        self.n[eng] += 1
        self.q[eng].append(("op", fn, self.n[eng]))
        return (eng, self.n[eng])

    def dma(self, eng, out, in_, semkey, deps=()):
        for d in deps:
            self.wait(eng, d)
        c = self.dcnt.get(semkey, 0) + 16
        self.dcnt[semkey] = c
        self.q[eng].append(("dma", out, in_, semkey))
        return (semkey, c)

    def last(self, eng):
        return (eng, self.n[eng]) if self.n[eng] > 0 else None


class Ring:
    def __init__(self, items):
        self.items = items
        self.i = 0
        self.free = [[] for _ in items]

    def get(self):
        idx = self.i % len(self.items)
        self.i += 1
        return idx, self.items[idx], list(self.free[idx])

    def release(self, idx, events):
        self.free[idx] = [e for e in events if e is not None]


def build_program(stop_after=None):
    nc = bass.Bass("TRN2", target_bir_lowering=False)

    def din(name, shape):
        return nc.dram_tensor(name, shape, F32, kind="ExternalInput").ap()

    def dout(name, shape):
        return nc.dram_tensor(name, shape, F32, kind="ExternalOutput").ap()

    xin = din("xin", [NT, D])
    cache = din("cache", [16, 30, D])
    pv_d = din("pv", [128, NPV])
    identf_d = din("identf", [128, 128])
    maskt_d = din("maskt", [128, 128])
    masks_d = din("masks", [64, 64])
    wst_d = din("wst", [128, 16, 128])
    wcol_d = din("wcol", [64, 16, 4])
    bsr_d = din("bsr", [1, D])
    lnvg_d = din("lnvg", [1, D])
    lnvb_d = din("lnvb", [1, D])
    w_gu1 = din("w_gu1", [D, 2 * DFF])
    w_dn1 = din("w_dn1", [DFF, D])
    w_in = din("w_in", [D, 6 * D])
    w_pa = din("w_pa", [D, D])
    w_pb = din("w_pb", [D, D])
    w_out = din("w_out", [D, D])
    w_gu2 = din("w_gu2", [D, 2 * DFF])
    w_dn2 = din("w_dn2", [DFF, D])

    y_o = dout("y", [1088, D])
    vlast_o = dout("vlast", [128, D])
    vsamp_o = dout("vsamp", [64, D])
    csp_o = dout("csp", [30, D])
    cssn_o = dout("cssn", [64, D])
    csso_o = dout("csso", [16, 26, D])

    P = Plan()

    with nc.sbuf_tensor("arena", [128, ARENA_B // 4], F32) as arena, \
            nc.psum_tensor("ps", [128, 8, 512], F32) as ps:

        def view(off, shape, dt=F32):
            esz = 4 if dt == F32 else 2
            n = int(np.prod(shape))
            a = arena[:, off // 4:(off + n * esz) // 4]
            if dt != F32:
                a = a.bitcast(dt)
            if len(shape) == 2:
                a = a.rearrange("p (a b) -> p a b", a=shape[0])
            elif len(shape) == 3:
                a = a.rearrange("p (a b c) -> p a b c", a=shape[0], b=shape[1])
            return a

        X = view(OFF_X, [KC, NT])
        PV = view(OFF_PV, [NPV])
        IDF = view(OFF_IDF, [128])
        ONES = view(OFF_ONES, [128], BF16)
        EPSV = view(OFF_EPS, [4])
        GTAIL = view(OFF_GTAIL, [KC, 30])
        WST = view(OFF_WST, [16, 128], BF16)
        MS = view(OFF_MS, [16, 64], BF16)
        SMALL = view(OFF_SMALL, [64])
        ST = SMALL[:, 0:24]
        MV = SMALL[:, 24:26]
        RS1 = SMALL[:, 28:29]
        RSTD = view(OFF_RSTD, [1152])
        slots = [view(OFF_SLOTS + i * SLOT_B, [4096], BF16) for i in range(NSLOT)]
        H = view(OFF_H, [KC, NT], BF16)
        HID = view(OFF_HID, [11, NT], BF16)
        HM = view(OFF_HM, [KC, 608], BF16)
        V = view(OFF_V, [5, D], BF16)
        MM_ = view(OFF_V, [KC, 576], BF16)
        US = view(OFF_US, [KC, 576], BF16)
        CN = view(OFF_CN, [KC, 576], BF16)

        banks = Ring([ps[:, b, :] for b in range(8)])
        wring = Ring(slots)

        def mm(out, lhsT, rhs, start, stop, deps=()):
            return P.op("pe", lambda e: e.matmul(out, lhsT, rhs, start=start, stop=stop), deps)

        def tr(out, in_, ident, deps=()):
            return P.op("pe", lambda e: e.transpose(out, in_, ident), deps)

        def act(out, in_, func, deps=(), **kw):
            return P.op("act", lambda e: e.activation(out=out, in_=in_, func=func, **kw), deps)

        def cp(eng, out, in_, deps=()):
            if eng == "act":
                return P.op("act", lambda e: e.copy(out=out, in_=in_), deps)
            return P.op(eng, lambda e: e.tensor_copy(out=out, in_=in_), deps)

        def tt(out, in0, in1, op, deps=(), eng="dve"):
            return P.op(eng, lambda e: e.tensor_tensor(out=out, in0=in0, in1=in1, op=op), deps)

        def ts(out, in0, s1, s2, op0, op1, deps=(), eng="dve"):
            return P.op(eng, lambda e: e.tensor_scalar(out=out, in0=in0, scalar1=s1, scalar2=s2,
                                                       op0=op0, op1=op1), deps)

        def stt(out, in0, scalar, in1, op0, op1, deps=(), eng="dve"):
            return P.op(eng, lambda e: e.scalar_tensor_tensor(out=out, in0=in0, scalar=scalar, in1=in1,
                                                              op0=op0, op1=op1), deps)

        def barrier():
            evs = [P.last(e) for e in ("pe", "act", "dve", "pool")]
            evs = [e for e in evs if e is not None]
            for ok_ in sorted(k for k in P.dcnt if k.startswith("o_")):
                evs.append((ok_, P.dcnt[ok_]))
            for e in ("pe", "act", "dve"):
                for ev in evs:
                    if ev[0] != e:
                        P.wait(e, ev)
            return evs

        w_hold = []

        def wload(parts, extra=()):
            idx, slot, free = wring.get()
            ev = None
            extra = list(extra) + list(w_hold)
            del w_hold[:]
            for i, (dst_fn, src) in enumerate(parts):
                ev = P.dma("pool", dst_fn(slot), src, "w%d" % idx, deps=(free + list(extra)) if i == 0 else ())
            return idx, slot, ev

        def wk(slot, n):
            return slot[:, 0:16 * n].rearrange("p (k n) -> p k n", n=n)

        def colblk(w, c0, n):
            return w[:, c0:c0 + n].rearrange("(k p) n -> p k n", p=128)

        MT = view(OFF_H, [128])
        MSK = view(OFF_H + 512, [64])
        WCOL = view(OFF_H + 1024, [16, 4])
        WSTT = view(OFF_H + 2048, [16, 128])
        c_ev = None
        for dst, src in ((PV, pv_d), (IDF, identf_d), (MT, maskt_d), (MSK[0:64], masks_d),
                         (WCOL[0:64], wcol_d), (WSTT, wst_d)):
            c_ev = P.dma("sp", dst, src, "c0")
        P.op("dve", lambda e: e.memset(ONES, 1.0))
        P.op("dve", lambda e: e.memset(EPSV, EPS))
        tt(WST, WSTT, MT.unsqueeze(1).to_broadcast([128, 16, 128]), ALU.mult, deps=[c_ev])
        tt(MS[0:64].rearrange("p h (i s) -> p h i s", s=16),
           WCOL[0:64].unsqueeze(3).to_broadcast([64, 16, 4, 16]),
           MSK[0:64].rearrange("p (i s) -> p i s", s=16).unsqueeze(1).to_broadcast([64, 16, 4, 16]),
           ALU.mult)
        barrier()

        XT = Ring([view(OFF_HID, [D]), view(OFF_HID + 8192, [D])])
        evq = 0
        for t in range(9):
            rows = 128 if t < 8 else 96
            i, xt, free = XT.get()
            lev = P.dma("sp", xt[0:rows], xin[t * 128:t * 128 + rows, :], "x%d" % i, deps=free)
            if t == 6:
                w_hold.append(lev)
            last_tr = None
            for g in range(4):
                bi, bk, bfree = banks.get()
                for kk in range(4):
                    k = g * 4 + kk
                    last_tr = tr(bk[:, kk * 128:kk * 128 + rows], xt[0:rows, k * 128:(k + 1) * 128],
                                 IDF[0:rows, 0:rows], deps=[lev, c_ev] + bfree)
                src = bk.rearrange("p (a b) -> p a b", b=128)[:, :, 0:rows]
                e = cp("act" if evq % 2 == 0 else "dve", X[:, g * 4:(g + 1) * 4, t * 128:t * 128 + rows], src,
                       deps=[last_tr])
                evq += 1
                banks.release(bi, [e])
            XT.release(i, [last_tr])
        P.dma("sp", csso_o, cache[:, 4:30, :], "out2")
        barrier()

        SQR = Ring([view(OFF_TMPX + 4096, [NT], BF16), view(OFF_TMPX + 4096 + 2240, [NT], BF16)])

        def rms(gidx, segs, Hout, xdeps=()):
            ready = {}
            for (x0, n, l0) in segs:
                bi, bk, bfree = banks.get()
                lastm = None
                for k in range(KC):
                    si, sq, sfree = SQR.get()
                    ea = act(sq[:, 0:n], X[:, k, x0:x0 + n], AF.Square, deps=list(xdeps) + sfree)
                    lastm = mm(bk[:, 0:n], ONES, sq[:, 0:n], k == 0, k == KC - 1,
                               deps=[ea] + (bfree if k == 0 else []))
                    SQR.release(si, [lastm])
                ea = act(RSTD[:, l0:l0 + n], bk[:, 0:n], AF.Sqrt, deps=[lastm], bias=EPSV[:, 0:1], scale=1.0 / D)
                banks.release(bi, [ea])
                er = P.op("dve", lambda e, l0=l0, n=n: e.reciprocal(out=RSTD[:, l0:l0 + n], in_=RSTD[:, l0:l0 + n]),
                          deps=[ea])
                ev = None
                for k in range(KC):
                    ev = stt(Hout[:, k, l0:l0 + n], X[:, k, x0:x0 + n], PV[:, gidx * 16 + k:gidx * 16 + k + 1],
                             RSTD[:, l0:l0 + n], ALU.mult, ALU.mult, deps=[er])
                ready[l0] = ev
            return ready

        FSEGS = [(0, 374, 0), (374, 374, 374), (748, 372, 748)]
        TMPS = Ring([view(OFF_TMPX, [512]), view(OFF_TMPX + 2048, [512])])

        def ffn(gidx, w_gu, w_dn):
            hready = rms(gidx, FSEGS, H)
            hid_free = []
            for gi in range(4):
                hid_ready = None
                for jj in range(11):
                    j = gi * 11 + jj
                    wi, slot, lev = wload([
                        (lambda s: wk(s, 256)[:, :, 0:128], colblk(w_gu, j * 128, 128)),
                        (lambda s: wk(s, 256)[:, :, 128:256], colblk(w_gu, DFF + j * 128, 128))])
                    W = wk(slot, 256)
                    lastu = None
                    for (x0, n, l0) in FSEGS:
                        gi_, bg, gfree = banks.get()
                        ui_, bu, ufree = banks.get()
                        lg = None
                        for k in range(KC):
                            lg = mm(bg[:, 0:n], W[:, k, 0:128], H[:, k, x0:x0 + n], k == 0, k == KC - 1,
                                    deps=[lev, hready[l0]] + (gfree if k == 0 else []))
                        for k in range(KC):
                            lastu = mm(bu[:, 0:n], W[:, k, 128:256], H[:, k, x0:x0 + n], k == 0, k == KC - 1,
                                       deps=(ufree if k == 0 else []))
                        ti, tmp, tfree = TMPS.get()
                        ea = act(tmp[:, 0:n], bg[:, 0:n], AF.Silu, deps=[lg] + tfree)
                        banks.release(gi_, [ea])
                        ed = tt(HID[:, jj, x0:x0 + n], tmp[:, 0:n], bu[:, 0:n], ALU.mult,
                                deps=[ea, lastu] + hid_free)
                        banks.release(ui_, [ed])
                        TMPS.release(ti, [ed])
                        hid_ready = ed
                    wring.release(wi, [lastu])
                lastd = None
                for mp in range(8):
                    wi, slot, lev = wload([
                        (lambda s: s[:, 0:11 * 256].rearrange("p (j n) -> p j n", n=256),
                         w_dn[gi * 1408:(gi + 1) * 1408, mp * 256:(mp + 1) * 256].rearrange("(j p) n -> p j n", p=128))])
                    W = slot[:, 0:11 * 256].rearrange("p (j n) -> p j n", n=256)
                    for mi in range(2):
                        m = mp * 2 + mi
                        for (x0, n, l0) in FSEGS:
                            bi, bk, bfree = banks.get()
                            for jj in range(11):
                                lastd = mm(bk[:, 0:n], W[:, jj, mi * 128:(mi + 1) * 128], HID[:, jj, x0:x0 + n],
                                           jj == 0, jj == 10, deps=[lev, hid_ready] + (bfree if jj == 0 else []))
                            ed = stt(X[:, m, x0:x0 + n], bk[:, 0:n], 0.5, X[:, m, x0:x0 + n], ALU.mult, ALU.add,
                                     deps=[lastd])
                            banks.release(bi, [ed])
                    wring.release(wi, [lastd])
                hid_free = [lastd]
            barrier()

        def mixer_pass(p0, samp):
            segs = [(p0, 512, 0)] + ([(1024, 96, 512)] if samp else [])
            bev = barrier()
            hm_ready = rms(PV_MIX, segs, HM)
            hm_all = list(hm_ready.values())
            LNG = view(OFF_CN, [D])
            LNB = view(OFF_CN + 8192, [D])
            ln_ev = P.dma("sp", LNG, lnvg_d.to_broadcast([128, D]), "ln", deps=bev)
            ln_ev = P.dma("sp", LNB, lnvb_d.to_broadcast([128, D]), "ln", deps=bev)
            tiles = [(t, 128, t * 128) for t in range(4)] + ([(4, 64, 544)] if samp else [])
            lastg = None
            gelu_ev = {}
            for nb in range(8):
                wi, slot, lev = wload([(lambda s: wk(s, 256), colblk(w_in, D + nb * 256, 256))])
                W = wk(slot, 256)
                lm = None
                for (t, rows, hc) in tiles:
                    bi, bk, bfree = banks.get()
                    for k in range(KC):
                        lm = mm(bk[0:rows, 0:256], HM[:, k, hc:hc + rows], W[:, k, :], k == 0, k == KC - 1,
                                deps=[lev] + hm_all + (bfree if k == 0 else []))
                    lastg = act(V[0:rows, t, nb * 256:(nb + 1) * 256], bk[0:rows, 0:256], AF.Gelu_apprx_tanh,
                                deps=[lm])
                    gelu_ev[t] = lastg
                    banks.release(bi, [lastg])
                wring.release(wi, [lm])
            VTR = Ring([view(OFF_US, [D]), view(OFF_US + 8192, [D])])
            small_free = [[], []]
            v_evs = []
            for ti_, (t, rows, hc) in enumerate(tiles):
                par = ti_ % 2
                STp = SMALL[:, par * 24:(par + 1) * 24]
                MVp = SMALL[:, 48 + par * 2:50 + par * 2]
                RSp = SMALL[:, 52 + par:53 + par]
                NMp = SMALL[:, 54 + par:55 + par]
                e0 = None
                for q in range(4):
                    e0 = P.op("dve", lambda e, q=q, t=t, rows=rows, STp=STp: e.bn_stats(
                        out=STp[0:rows, q * 6:(q + 1) * 6], in_=V[0:rows, t, q * 512:(q + 1) * 512]),
                        deps=[gelu_ev[t]] + small_free[par])
                e1 = P.op("dve", lambda e, rows=rows, STp=STp, MVp=MVp: e.bn_aggr(out=MVp[0:rows, 0:2],
                                                                                  in_=STp[0:rows, 0:24]), deps=[e0])
                e2 = act(RSp[0:rows], MVp[0:rows, 1:2], AF.Sqrt, deps=[e1], bias=EPSV[0:rows, 0:1], scale=1.0)
                e3 = P.op("dve", lambda e, rows=rows, RSp=RSp: e.reciprocal(out=RSp[0:rows], in_=RSp[0:rows]),
                          deps=[e2])
                e3b = stt(NMp[0:rows], MVp[0:rows, 0:1], -1.0, RSp[0:rows], ALU.mult, ALU.mult, deps=[e3])
                is_samp = (t == 4)
                is_out = is_samp or (t == 3 and p0 == 512)
                if is_out:
                    vi, VTt, vfree = VTR.get()
                    e4 = act(VTt[0:rows], V[0:rows, t, :], AF.Identity, deps=[e3b] + vfree,
                             scale=RSp[0:rows, 0:1], bias=NMp[0:rows, 0:1])
                    if not is_samp:
                        e4b = act(V[0:rows, t, :], V[0:rows, t, :], AF.Identity, deps=[e4],
                                  scale=RSp[0:rows, 0:1], bias=NMp[0:rows, 0:1])
                        v_evs.append(e4b)
                        small_free[par] = [e4b]
                    else:
                        small_free[par] = [e4]
                    e5 = tt(VTt[0:rows], VTt[0:rows], LNG[0:rows], ALU.mult, deps=[e4, ln_ev])
                    e6 = tt(VTt[0:rows], VTt[0:rows], LNB[0:rows], ALU.add, deps=[e5])
                    rel = [e6]
                    if is_samp:
                        e7 = cp("dve", V[0:rows, t, :], VTt[0:rows], deps=[e6])
                        v_evs.append(e7)
                        rel.append(e7)
                    od = P.dma("sp", vsamp_o if is_samp else vlast_o, VTt[0:rows], "o_vt%d" % vi, deps=[e6])
                    VTR.release(vi, rel + [od])
                else:
                    e4 = act(V[0:rows, t, :], V[0:rows, t, :], AF.Identity, deps=[e3b],
                             scale=RSp[0:rows, 0:1], bias=NMp[0:rows, 0:1])
                    small_free[par] = [e4]
                    v_evs.append(e4)
            bev = barrier()
            BSB = view(OFF_CN, [D])
            UR = Ring([view(OFF_CN + 8192, [512]), view(OFF_CN + 8192 + 2048, [512])])
            T1R = Ring([view(OFF_CN + 12288, [512]), view(OFF_CN + 12288 + 2048, [512])])
            bs_ev0 = P.dma("sp", BSB, bsr_d.to_broadcast([128, D]), "ln", deps=bev)
            BSS = view(OFF_CN + 16384, [16, 4])
            e_bss = cp("dve", BSS, BSB.rearrange("p (h i) -> p h i", i=128)[:, :, 0:4], deps=[bs_ev0])
            bs_ev = e_bss
            for q in range(4):
                rbi, rbk, rbfree = banks.get()
                erw = mm(rbk[:, 0:512], ONES, WST[:, 4 * q:4 * q + 4, :].rearrange("p a b -> p (a b)"), True, True,
                         deps=rbfree)
                for hh in range(4):
                    h = 4 * q + hh
                    bs_ev = stt(BSB[:, h * 128:(h + 1) * 128], rbk[:, hh * 128:(hh + 1) * 128],
                                PV[:, PV_LVB * 16 + h:PV_LVB * 16 + h + 1], BSB[:, h * 128:(h + 1) * 128],
                                ALU.mult, ALU.add, deps=[erw, e_bss])
                banks.release(rbi, [bs_ev])
            us_last = None
            for cpair in range(8):
                wi, slot, lev = wload([(lambda s: wk(s, 256), colblk(w_in, cpair * 256, 256))])
                W = wk(slot, 256)
                lm = None
                for ci in range(2):
                    c = cpair * 2 + ci
                    for (x0, n, l0) in segs:
                        bi, bk, bfree = banks.get()
                        for k in range(KC):
                            lm = mm(bk[:, 0:n], W[:, k, ci * 128:(ci + 1) * 128], HM[:, k, l0:l0 + n],
                                    k == 0, k == KC - 1, deps=[lev] + (bfree if k == 0 else []))
                        ui, u, ufree = UR.get()
                        eu = act(u[:, 0:n], bk[:, 0:n], AF.Gelu_apprx_tanh, deps=[lm] + ufree)
                        banks.release(bi, [eu])
                        b2i, b2, b2free = banks.get()
                        t1i, t1, t1free = T1R.get()
                        if n == 512:
                            lg = None
                            for t in range(4):
                                lg = mm(b2[:, t * 128:(t + 1) * 128], V[:, t, c * 128:(c + 1) * 128], WST[:, c, :],
                                        True, True, deps=(b2free if t == 0 else []))
                            e1 = stt(t1[:, 0:512].rearrange("p (t i) -> p t i", i=128),
                                     b2[:, 0:512].rearrange("p (t i) -> p t i", i=128),
                                     PV[:, PV_LVG * 16 + c:PV_LVG * 16 + c + 1],
                                     BSB[:, c * 128:(c + 1) * 128].unsqueeze(1).to_broadcast([128, 4, 128]),
                                     ALU.mult, ALU.add, deps=[lg, bs_ev] + t1free)
                            banks.release(b2i, [e1])
                            us_last = tt(US[:, c, 0:512], t1[:, 0:512], u[:, 0:512], ALU.mult, deps=[e1, eu])
                        else:
                            lg = mm(b2[:, 0:64], V[0:64, 4, c * 128:(c + 1) * 128], MS[0:64, c, :], True, True,
                                    deps=b2free)
                            e1 = tt(t1[:, 0:64].rearrange("p (i s) -> p i s", s=16),
                                    b2[:, 0:64].rearrange("p (i s) -> p i s", s=16),
                                    BSS[:, c, :].unsqueeze(2).to_broadcast([128, 4, 16]),
                                    ALU.add, deps=[lg, bs_ev] + t1free)
                            banks.release(b2i, [e1])
                            us_last = tt(US[:, c, 512:576], t1[:, 0:64], u[:, 32:96], ALU.mult, deps=[e1, eu])
                        UR.release(ui, [us_last])
                        T1R.release(t1i, [us_last])
                wring.release(wi, [lm])
            bev = barrier()
            o = OFF_V
            GB = []
            for _i in range(2):
                GB.append(dict(GPB=view(o, [544], BF16), GSB=view(o + 1088, [16, 34], BF16),
                               GT=view(o + 2176, [96]), GPF=view(o + 2560, [32]), free=[]))
                o += 2688
            TSR = Ring([view(o, [512]), view(o + 2048, [512])]); o += 4096
            CT0 = view(o, [4, 128]); o += 2048
            ACC = CT0.rearrange("p a b -> p (a b)")
            CSTR = Ring([view(o, [128]), view(o + 512, [128])]); o += 1024
            DG0 = view(o, [31, 128], BF16); o += 7936
            assert o <= OFF_V + 20480
            DGR = Ring([DG0, view(OFF_TMPX, [31, 128], BF16)])
            SQC = Ring([view(OFF_TMPX + 7936, [576], BF16), view(OFF_TMPX + 7936 + 1152, [576], BF16)])
            CTR = Ring([CT0, view(OFF_TMPX + 10240, [4, 128])])
            CTR.free = [list(bev), list(bev)]
            GTB = view(OFF_GTAIL, [KC, 30], BF16)
            bstate = {"ct_free": list(bev)}

            def glu_part(c):
                G = GB[c % 2]
                GPB, GSB, GT, GPF, gfree = G["GPB"], G["GSB"], G["GT"], G["GPF"], list(G["free"])
                wi, slot, lev = wload([
                    (lambda s: wk(s, 256)[:, :, 0:128], colblk(w_in, 2 * D + c * 128, 128)),
                    (lambda s: wk(s, 256)[:, :, 128:256], colblk(w_in, 3 * D + c * 128, 128))])
                W = wk(slot, 256)
                di, DG, dfree = DGR.get()
                edg = tt(DG, IDF.unsqueeze(1).to_broadcast([128, 31, 128]),
                         PV[:, PV_WDW + c * 31:PV_WDW + (c + 1) * 31].unsqueeze(2).to_broadcast([128, 31, 128]),
                         ALU.mult, deps=dfree)
                egs = None
                if samp:
                    cti, CT, ctfree = CTR.get()
                    cev = P.dma("pool", CT[0:120],
                                cache[:, :, c * 128:(c + 1) * 128].rearrange("(q s) r n -> (s r) q n", q=4),
                                "ct%d" % cti, deps=ctfree)
                    bi, bk, bfree = banks.get()
                    ltr = None
                    for q in range(4):
                        ltr = tr(bk[:, q * 120:(q + 1) * 120], CT[0:120, q, :], IDF[0:120, 0:120],
                                 deps=[cev] + (bfree if q == 0 else []))
                    CTR.release(cti, [ltr])
                    egs = cp("act", GSB[:, :, 0:30], bk[:, 0:480].rearrange("p (s r) -> p s r", r=30),
                             deps=[ltr] + gfree)
                    banks.release(bi, [egs])
                lm = None
                g_evs = []
                for (x0, n, l0) in segs:
                    ai, ba, afree = banks.get()
                    bbi, bb, bbfree = banks.get()
                    la = None
                    for k in range(KC):
                        la = mm(ba[:, 0:n], W[:, k, 0:128], HM[:, k, l0:l0 + n], k == 0, k == KC - 1,
                                deps=[lev] + (afree if k == 0 else []))
                    for k in range(KC):
                        lm = mm(bb[:, 0:n], W[:, k, 128:256], HM[:, k, l0:l0 + n], k == 0, k == KC - 1,
                                deps=(bbfree if k == 0 else []))
                    ti, tsg, tfree = TSR.get()
                    ea = act(tsg[:, 0:n], bb[:, 0:n], AF.Sigmoid, deps=[lm] + tfree)
                    banks.release(bbi, [ea])
                    if n == 512:
                        eg = tt(GPB[:, 30:542], tsg[:, 0:512], ba[:, 0:512], ALU.mult, deps=[ea, la] + gfree)
                        if not samp:
                            g_evs.append(eg)
                            eg = tt(GPF[:, 0:32], tsg[:, 480:512], ba[:, 480:512], ALU.mult, deps=[ea, la] + gfree)
                    else:
                        eg = tt(GT[:, 0:96], tsg[:, 0:96], ba[:, 0:96], ALU.mult, deps=[ea, la] + gfree)
                    banks.release(ai, [eg])
                    TSR.release(ti, [eg])
                    g_evs.append(eg)
                wring.release(wi, [lm])
                if samp:
                    eh = cp("dve", GPB[:, 0:30], GT[:, 2:32], deps=g_evs + gfree)
                    eh2 = cp("dve", GSB[:, :, 30:34].rearrange("p s t -> p t s"),
                             GT[:, 32:96].rearrange("p (t s) -> p t s", s=16), deps=g_evs + [egs] + gfree)
                    eh3 = cp("dve", GTB[:, c, :], GPB[:, 512:542], deps=g_evs)
                    hist = [eh, eh2, eh3]
                else:
                    eh = cp("dve", GPB[:, 0:30], GTB[:, c, :], deps=g_evs + gfree)
                    hist = [eh]
                return dict(G=G, di=di, DG=DG, ready=[edg] + g_evs + hist, g_evs=g_evs)

            def conv_part(c, st):
                G = st["G"]
                GPB, GSB, GT, GPF = G["GPB"], G["GSB"], G["GT"], G["GPF"]
                DG = st["DG"]
                bcol = PV[:, PV_BDW * 16 + c:PV_BDW * 16 + c + 1]
                bci, bc, bcfree = banks.get()
                lc = None
                for k in range(31):
                    lc = mm(bc[:, 0:512], DG[:, k, :], GPB[:, k:k + 512], k == 0, k == 30,
                            deps=st["ready"] + (bcfree if k == 0 else []))
                last = act(CN[:, c, 0:512], bc[:, 0:512], AF.Identity, deps=[lc], bias=bcol, scale=1.0)
                banks.release(bci, [last])
                readers = [lc]
                if samp:
                    bsi, bs2, bsfree = banks.get()
                    ls = None
                    for k in range(31):
                        ls = mm(bs2[:, 0:64].rearrange("p (s t) -> p s t", t=4), DG[:, k, :], GSB[:, :, k:k + 4],
                                k == 0, k == 30, deps=(bsfree if k == 0 else []))
                    last = act(CN[:, c, 512:576].rearrange("p (t s) -> p s t", t=4),
                               bs2[:, 0:64].rearrange("p (s t) -> p s t", t=4), AF.Identity, deps=[ls],
                               bias=bcol, scale=1.0)
                    banks.release(bsi, [last])
                    readers.append(ls)
                DGR.release(st["di"], [readers[-1]])
                bi, bk, bfree = banks.get()
                si, cst, sfree = CSTR.get()
                if samp:
                    etr = tr(bk[0:64, 0:128], GT[:, 32:96], IDF, deps=st["g_evs"] + bfree)
                    ecs = cp("act", cst[0:64], bk[0:64, 0:128], deps=[etr] + sfree)
                    od = P.dma("sp", cssn_o[:, c * 128:(c + 1) * 128], cst[0:64], "o_cs%d" % si, deps=[ecs])
                else:
                    etr = tr(bk[0:32, 0:128], GPF[:, 0:32], IDF, deps=st["g_evs"] + bfree)
                    ecs = cp("act", cst[0:32], bk[0:32, 0:128], deps=[etr] + sfree)
                    od = P.dma("sp", csp_o[:, c * 128:(c + 1) * 128], cst[2:32], "o_cs%d" % si, deps=[ecs])
                banks.release(bi, [ecs])
                CSTR.release(si, [od])
                G["free"] = readers + [etr]
                return last

            cn_last = None
            st_cur = glu_part(0)
            for c in range(KC):
                st_next = glu_part(c + 1) if c + 1 < KC else None
                cn_last = conv_part(c, st_cur)
                st_cur = st_next
            MEAN = view(OFF_RSTD, [576])
            RSC = view(OFF_RSTD + 2304, [576])
            TBR = Ring([view(OFF_V, [576]), view(OFF_V + 2304, [576])])
            gb_dead = GB[0]["free"] + GB[1]["free"] + [cn_last]
            nc_ = 576 if samp else 512
            cseg = [(0, 512)] + ([(512, 64)] if samp else [])
            sbanks = []
            for (l0, n) in cseg:
                si_, bs_, sfree_ = banks.get()
                qi_, bq_, qfree_ = banks.get()
                sbanks.append((si_, bs_, sfree_, qi_, bq_, qfree_))
            lq = None
            for k in range(KC):
                sqi, sq, sqfree = SQC.get()
                ea = act(sq[:, 0:nc_], CN[:, k, 0:nc_], AF.Square, deps=[cn_last] + sqfree)
                for (l0, n), (si_, bs_, sfree_, qi_, bq_, qfree_) in zip(cseg, sbanks):
                    mm(bs_[:, 0:n], ONES, CN[:, k, l0:l0 + n], k == 0, k == KC - 1,
                       deps=[cn_last] + (sfree_ if k == 0 else []))
                    lq = mm(bq_[:, 0:n], ONES, sq[:, l0:l0 + n], k == 0, k == KC - 1,
                            deps=[ea] + (qfree_ if k == 0 else []))
                SQC.release(sqi, [lq])
            e3 = None
            for (l0, n), (si_, bs_, sfree_, qi_, bq_, qfree_) in zip(cseg, sbanks):
                e1 = ts(MEAN[:, l0:l0 + n], bs_[:, 0:n], 1.0 / D, 0.0, ALU.mult, ALU.add, deps=[lq])
                banks.release(si_, [e1])
                e2 = stt(ACC[:, 0:n], MEAN[:, l0:l0 + n], -1.0, MEAN[:, l0:l0 + n], ALU.mult, ALU.mult, deps=[e1])
                e3 = stt(RSC[:, l0:l0 + n], bq_[:, 0:n], 1.0 / D, ACC[:, 0:n], ALU.mult, ALU.add, deps=[e2])
                banks.release(qi_, [e3])
            e4 = act(RSC[:, 0:nc_], RSC[:, 0:nc_], AF.Sqrt, deps=[e3], bias=EPSV[:, 0:1], scale=1.0)
            e5 = P.op("dve", lambda e: e.reciprocal(out=RSC[:, 0:nc_], in_=RSC[:, 0:nc_]), deps=[e4])
            for k in range(KC):
                ti, tb, tfree = TBR.get()
                e6 = tt(tb[:, 0:nc_], CN[:, k, 0:nc_], MEAN[:, 0:nc_], ALU.subtract, deps=[e5] + tfree + gb_dead)
                e7 = tt(tb[:, 0:nc_], tb[:, 0:nc_], RSC[:, 0:nc_], ALU.mult, deps=[e6])
                e8 = act(CN[:, k, 0:nc_], tb[:, 0:nc_], AF.Silu, deps=[e7],
                         scale=PV[:, PV_LCG * 16 + k:PV_LCG * 16 + k + 1],
                         bias=PV[:, PV_LCB * 16 + k:PV_LCB * 16 + k + 1])
                TBR.release(ti, [e8])
            barrier()
            SAR = Ring([view(OFF_TMPX, [512]), view(OFF_TMPX + 2048, [512])])
            SBR = Ring([view(OFF_TMPX + 4096, [512]), view(OFF_TMPX + 6144, [512])])
            TA = view(OFF_TMPX + 8192, [512])
            TB = view(OFF_TMPX + 10240, [512])
            msegs = [(0, 512, 0, p0)] + ([(512, 64, 544, 1056)] if samp else [])
            m_last = None
            for c in range(KC):
                w1i, slot1, lev1 = wload([
                    (lambda s: wk(s, 256)[:, :, 0:128], colblk(w_pa, c * 128, 128)),
                    (lambda s: wk(s, 256)[:, :, 128:256], colblk(w_pb, c * 128, 128))])
                w2i, slot2, lev2 = wload([
                    (lambda s: wk(s, 256)[:, :, 0:128], colblk(w_in, 4 * D + c * 128, 128)),
                    (lambda s: wk(s, 256)[:, :, 128:256], colblk(w_in, 5 * D + c * 128, 128))])
                W1 = wk(slot1, 256)
                W2 = wk(slot2, 256)
                l1 = l2 = None
                for (l0, n, hl0, xc) in msegs:
                    yai, bya, f1 = banks.get()
                    ybi, byb, f2 = banks.get()
                    gai, bga, f3 = banks.get()
                    gbi, bgb, f4 = banks.get()
                    lya = lyb = lga = lgb = None
                    for k in range(KC):
                        lya = mm(bya[:, 0:n], W1[:, k, 0:128], US[:, k, l0:l0 + n], k == 0, k == KC - 1,
                                 deps=[lev1] + (f1 if k == 0 else []))
                    for k in range(KC):
                        lyb = mm(byb[:, 0:n], W1[:, k, 128:256], CN[:, k, l0:l0 + n], k == 0, k == KC - 1,
                                 deps=(f2 if k == 0 else []))
                    for k in range(KC):
                        lga = mm(bga[:, 0:n], W2[:, k, 0:128], HM[:, k, hl0:hl0 + n], k == 0, k == KC - 1,
                                 deps=[lev2] + (f3 if k == 0 else []))
                    for k in range(KC):
                        lgb = mm(bgb[:, 0:n], W2[:, k, 128:256], HM[:, k, hl0:hl0 + n], k == 0, k == KC - 1,
                                 deps=(f4 if k == 0 else []))
                    l1, l2 = lyb, lgb
                    sai, sa, saf = SAR.get()
                    sbi, sb, sbf = SBR.get()
                    ea = act(sa[:, 0:n], bga[:, 0:n], AF.Sigmoid, deps=[lga] + saf)
                    eb = act(sb[:, 0:n], bgb[:, 0:n], AF.Sigmoid, deps=[lgb] + sbf)
                    banks.release(gai, [ea])
                    banks.release(gbi, [eb])
                    e1 = tt(TA[:, 0:n], sa[:, 0:n], bya[:, 0:n], ALU.mult, deps=[ea, lya] + ([m_last] if m_last else []))
                    e2 = tt(TB[:, 0:n], sb[:, 0:n], byb[:, 0:n], ALU.mult, deps=[eb, lyb])
                    banks.release(yai, [e1])
                    banks.release(ybi, [e2])
                    m_last = tt(MM_[:, c, l0:l0 + n], TA[:, 0:n], TB[:, 0:n], ALU.add, deps=[e1, e2])
                    SAR.release(sai, [e1])
                    SBR.release(sbi, [e2])
                wring.release(w1i, [l1])
                wring.release(w2i, [l2])
            for mp in range(8):
                wi, slot, lev = wload([(lambda s: wk(s, 256), colblk(w_out, mp * 256, 256))])
                W = wk(slot, 256)
                lm = None
                for mi in range(2):
                    m = mp * 2 + mi
                    for (l0, n, hl0, xc) in msegs:
                        bi, bk, bfree = banks.get()
                        for k in range(KC):
                            lm = mm(bk[:, 0:n], W[:, k, mi * 128:(mi + 1) * 128], MM_[:, k, l0:l0 + n],
                                    k == 0, k == KC - 1, deps=[lev, m_last] + (bfree if k == 0 else []))
                        ed = tt(X[:, m, xc:xc + n], bk[:, 0:n], X[:, m, xc:xc + n], ALU.add, deps=[lm])
                        banks.release(bi, [ed])
                wring.release(wi, [lm])
            barrier()

        if stop_after != "p0":
            ffn(PV_F1, w_gu1, w_dn1)
        if stop_after not in ("p0", "ffn1"):
            mixer_pass(0, True)
            mixer_pass(512, False)
        if stop_after not in ("p0", "ffn1", "mixer"):
            ffn(PV_F2, w_gu2, w_dn2)

        barrier()
        seg_ready = {}
        if stop_after is None:
            OSEGS = [(0, 512, 0), (512, 512, 512), (1056, 64, 1056)]
            for si_, (x0, n, l0) in enumerate(OSEGS):
                bi, bk, bfree = banks.get()
                lastm = None
                for k in range(KC):
                    si, sq, sfree = SQR.get()
                    ea = act(sq[:, 0:n], X[:, k, x0:x0 + n], AF.Square, deps=sfree)
                    lastm = mm(bk[:, 0:n], ONES, sq[:, 0:n], k == 0, k == KC - 1, deps=[ea] + (bfree if k == 0 else []))
                    SQR.release(si, [lastm])
                ea = act(RSTD[:, l0:l0 + n], bk[:, 0:n], AF.Sqrt, deps=[lastm], bias=EPSV[:, 0:1], scale=1.0 / D)
                banks.release(bi, [ea])
                er = P.op("dve", lambda e, n=n, l0=l0: e.reciprocal(out=RSTD[:, l0:l0 + n], in_=RSTD[:, l0:l0 + n]),
                          deps=[ea])
                er2 = None
                for k in range(KC):
                    er2 = stt(X[:, k, x0:x0 + n], X[:, k, x0:x0 + n],
                              PV[:, PV_FIN * 16 + k:PV_FIN * 16 + k + 1], RSTD[:, l0:l0 + n], ALU.mult, ALU.mult,
                              deps=[er])
                seg_ready[si_] = er2
        YT = Ring([view(OFF_HID, [D]), view(OFF_HID + 8192, [D])])
        evq = 0
        for t in range(9):
            rows = 128 if t < 8 else 64
            c0 = t * 128 if t < 8 else 1056
            sdep = seg_ready.get(0 if t < 4 else (1 if t < 8 else 2))
            sdep = [sdep] if sdep else []
            i, yt, free = YT.get()
            evs = []
            for g in range(4):
                bi, bk, bfree = banks.get()
                ltr = None
                for kk in range(4):
                    k = g * 4 + kk
                    ltr = tr(bk[0:rows, kk * 128:(kk + 1) * 128], X[:, k, c0:c0 + rows], IDF,
                             deps=sdep + (bfree if kk == 0 else []))
                e = cp("act" if evq % 2 == 0 else "dve", yt[0:rows, g * 512:(g + 1) * 512], bk[0:rows, :],
                       deps=[ltr] + free)
                evq += 1
                banks.release(bi, [e])
                evs.append(e)
            od = P.dma("sp", y_o[t * 128:t * 128 + rows, :], yt[0:rows], "o_y%d" % i, deps=evs)
            YT.release(i, [od])
        for ok_ in sorted(k for k in P.dcnt if k.startswith("o_")):
            P.wait("sp", (ok_, P.dcnt[ok_]))
        P.wait("sp", ("out2", P.dcnt["out2"]))

        dkeys = sorted(P.dcnt.keys())
        import contextlib
        with contextlib.ExitStack() as stack:
            esem = {e: stack.enter_context(nc.semaphore("s_" + e)) for e in ("pe", "act", "dve", "pool")}
            dsem = {k: stack.enter_context(nc.semaphore("d_" + k)) for k in dkeys}
            block = stack.enter_context(nc.Block())
            sigval = {}
            for e in ("pe", "act", "dve", "pool"):
                need = sorted(P.needed[e])
                sigval[e] = {idx: i + 1 for i, idx in enumerate(need)}

            def emit(eng, e):
                for item in P.q[eng]:
                    if item[0] == "wait":
                        k, v = item[1], item[2]
                        if k in esem:
                            e.wait_ge(esem[k], sigval[k][v])
                        else:
                            e.wait_ge(dsem[k], v)
                    elif item[0] == "op":
                        ins = item[1](e)
                        idx = item[2]
                        if eng in sigval and idx in sigval[eng]:
                            ins.then_inc(esem[eng], 1)
                    else:
                        ins = e.dma_start(out=item[1], in_=item[2])
                        ins.then_inc(dsem[item[3]], 16)

            @block.tensor
            def _(e):
                emit("pe", e)

            @block.scalar
            def _(e):
                emit("act", e)

            @block.vector
            def _(e):
                emit("dve", e)

            @block.gpsimd
            def _(e):
                emit("pool", e)

            @block.sync
            def _(e):
                emit("sp", e)
    return nc


_CACHE = {}


def _consts():
    identf = np.eye(128, dtype=np.float32)
    jj = np.arange(128)
    maskt = (jj[:, None] <= jj[None, :]).astype(np.float32)
    p = np.arange(64)
    pj, ps_ = p // 16, p % 16
    masks = ((ps_[:, None] == ps_[None, :]) & (pj[:, None] <= pj[None, :])).astype(np.float32)
    return identf, maskt, masks


def kernel(x_prompt, x_sample, cache_conv, ffn1_norm, ffn1_w_gu, ffn1_w_down, mix_norm,
           w_in, w_s, b_s, ln_v_g, ln_v_b, w_pa, w_dw, b_dw, ln_c_g, ln_c_b, w_pb, w_out,
           ffn2_norm, ffn2_w_gu, ffn2_w_down, final_norm, _stop_after=None):
    f32 = np.float32
    A = lambda a: np.ascontiguousarray(np.asarray(a, dtype=f32))
    x_prompt, x_sample, cache_conv = A(x_prompt), A(x_sample), A(cache_conv)
    key = _stop_after
    if key not in _CACHE:
        _CACHE[key] = build_program(_stop_after)
    nc = _CACHE[key]

    identf, maskt, masks = _consts()
    vecs = [ffn1_norm[0], mix_norm[0], ffn2_norm[0], final_norm, b_dw[0], ln_c_g[0], ln_c_b[0], ln_v_g[0], ln_v_b[0]]
    pv = np.zeros((128, NPV), f32)
    for vi, v in enumerate(vecs):
        pv[:, vi * 16:(vi + 1) * 16] = A(v).reshape(16, 128).T
    wd = A(w_dw[0])
    pv[:, PV_WDW:] = wd.reshape(31, 16, 128).transpose(2, 1, 0).reshape(128, 16 * 31)
    ws = A(w_s[0])
    wst = np.ascontiguousarray(ws.transpose(2, 0, 1))
    wcol = np.ascontiguousarray(np.repeat(ws[:, 0:4, 0:4].transpose(2, 0, 1), 16, axis=0))
    bsr = A(b_s[0]).reshape(1, D)
    lnvg = A(ln_v_g[0]).reshape(1, D)
    lnvb = A(ln_v_b[0]).reshape(1, D)
    shared = {
        "pv": pv, "identf": identf, "maskt": maskt, "masks": masks, "wst": wst, "wcol": wcol,
        "bsr": bsr, "lnvg": lnvg, "lnvb": lnvb,
        "w_gu1": A(ffn1_w_gu[0]), "w_dn1": A(ffn1_w_down[0]), "w_in": A(w_in[0]),
        "w_pa": A(w_pa[0]), "w_pb": A(w_pb[0]), "w_out": A(w_out[0]),
        "w_gu2": A(ffn2_w_gu[0]), "w_dn2": A(ffn2_w_down[0]),
    }
    in_maps = []
    for c in range(8):
        b, hf = c // 2, c % 2
        xc = np.zeros((NT, D), f32)
        xc[0:1024] = x_prompt[b, hf * 1024:(hf + 1) * 1024]
        if hf == 1:
            xc[1024:1056] = x_prompt[b, 992:1024]
        xs = x_sample[c * 16:(c + 1) * 16]
        xc[1056:1120] = xs.transpose(1, 0, 2).reshape(64, D)
        m = dict(shared)
        m["xin"] = xc
        m["cache"] = np.ascontiguousarray(cache_conv[0, c * 16:(c + 1) * 16])
        in_maps.append(m)

    res = run_bass_kernel_spmd(nc, in_maps, core_ids=list(range(8)))
    R = res.results

    y_prompt = np.zeros((4, 2048, D), f32)
    y_sample = np.zeros((128, 4, D), f32)
    cvp = np.zeros((1, 4, 128, D), f32)
    cvs = np.zeros((1, 128, 4, D), f32)
    csp = np.zeros((1, 4, 30, D), f32)
    css = np.zeros((1, 128, 30, D), f32)
    for c in range(8):
        b, hf = c // 2, c % 2
        r = R[c]
        y_prompt[b, hf * 1024:(hf + 1) * 1024] = r["y"][0:1024]
        y_sample[c * 16:(c + 1) * 16] = r["y"][1024:1088].reshape(4, 16, D).transpose(1, 0, 2)
        cvs[0, c * 16:(c + 1) * 16] = r["vsamp"].reshape(4, 16, D).transpose(1, 0, 2)
        css[0, c * 16:(c + 1) * 16, 0:26] = r["csso"]
        css[0, c * 16:(c + 1) * 16, 26:30] = r["cssn"].reshape(4, 16, D).transpose(1, 0, 2)
        if hf == 1:
            cvp[0, b] = r["vlast"]
            csp[0, b] = r["csp"]
    return (y_prompt, y_sample, cvp, cvs, csp, css)
```

```python
import numpy as np
import concourse.bass as bass
import concourse.mybir as mybir
from concourse.bass_utils import run_bass_kernel_spmd

F32 = mybir.dt.float32
BF16 = mybir.dt.bfloat16
AF = mybir.ActivationFunctionType
ALU = mybir.AluOpType

D = 2048
KC = 16
DFF = 5632
NT = 1120
NPV = 9 * 16 + 31 * 16
PV_F1, PV_MIX, PV_F2, PV_FIN, PV_BDW, PV_LCG, PV_LCB, PV_LVG, PV_LVB = range(9)
PV_WDW = 9 * 16
EPS = 1e-6
ENGS = ["pe", "act", "dve", "pool", "sp"]
NSLOT = 4
SLOT_B = 8192

OFF_X = 0
OFF_PV = 71680
OFF_IDF = OFF_PV + 2560
OFF_ONES = OFF_IDF + 512
OFF_EPS = OFF_ONES + 256
OFF_GTAIL = OFF_EPS + 16
OFF_WST = OFF_GTAIL + 1920
OFF_MS = OFF_WST + 4096
OFF_SMALL = OFF_MS + 2048
PL0 = 83456
assert OFF_SMALL + 256 <= PL0
OFF_RSTD = PL0
OFF_SLOTS = OFF_RSTD + 4608
OFF_TMPX = OFF_SLOTS + NSLOT * SLOT_B
REG0 = OFF_TMPX + 12288
OFF_H = REG0
OFF_HID = REG0 + 35840
OFF_HM = REG0
OFF_V = REG0 + 19456
OFF_US = OFF_V + 20480
OFF_CN = OFF_US + 18432
ARENA_B = REG0 + 78848


class Plan:
    def __init__(self):
        self.q = {e: [] for e in ENGS}
        self.n = {e: 0 for e in ENGS}
        self.waited = {e: {} for e in ENGS}
        self.dcnt = {}
        self.needed = {e: set() for e in ENGS}

    def wait(self, eng, ev):
        if ev is None:
            return
        k, v = ev
        if self.waited[eng].get(k, 0) >= v:
            return
        self.waited[eng][k] = v
        if k in self.needed:
            self.needed[k].add(v)
        self.q[eng].append(("wait", k, v))

    def op(self, eng, fn, deps=()):
        for d in deps:
            self.wait(eng, d)
        self.n[eng] += 1
        self.q[eng].append(("op", fn, self.n[eng]))
        return (eng, self.n[eng])

    def dma(self, eng, out, in_, semkey, deps=()):
        for d in deps:
            self.wait(eng, d)
        c = self.dcnt.get(semkey, 0) + 16
        self.dcnt[semkey] = c
        self.q[eng].append(("dma", out, in_, semkey))
        return (semkey, c)

    def last(self, eng):
        return (eng, self.n[eng]) if self.n[eng] > 0 else None


class Ring:
    def __init__(self, items):
        self.items = items
        self.i = 0
        self.free = [[] for _ in items]

    def get(self):
        idx = self.i % len(self.items)
        self.i += 1
        return idx, self.items[idx], list(self.free[idx])

    def release(self, idx, events):
        self.free[idx] = [e for e in events if e is not None]


def build_program(stop_after=None):
    nc = bass.Bass("TRN2", target_bir_lowering=False)

    def din(name, shape):
        return nc.dram_tensor(name, shape, F32, kind="ExternalInput").ap()

    def dout(name, shape):
        return nc.dram_tensor(name, shape, F32, kind="ExternalOutput").ap()

    xin = din("xin", [NT, D])
    cache = din("cache", [16, 30, D])
    pv_d = din("pv", [128, NPV])
    identf_d = din("identf", [128, 128])
    maskt_d = din("maskt", [128, 128])
    masks_d = din("masks", [64, 64])
    wst_d = din("wst", [128, 16, 128])
    wcol_d = din("wcol", [64, 16, 4])
    bsr_d = din("bsr", [1, D])
    lnvg_d = din("lnvg", [1, D])
    lnvb_d = din("lnvb", [1, D])
    w_gu1 = din("w_gu1", [44, 128, 4096])
    w_dn1 = din("w_dn1", [32, 128, 2816])
    w_v = din("w_v", [8, 128, 4096])
    w_u = din("w_u", [8, 128, 4096])
    w_glu = din("w_glu", [16, 128, 4096])
    w_gate = din("w_gate", [16, 128, 4096])
    w_pab = din("w_pab", [16, 128, 4096])
    w_out = din("w_out", [8, 128, 4096])
    w_gu2 = din("w_gu2", [44, 128, 4096])
    w_dn2 = din("w_dn2", [32, 128, 2816])

    y_o = dout("y", [1088, D])
    vlast_o = dout("vlast", [128, D])
    vsamp_o = dout("vsamp", [64, D])
    csp_o = dout("csp", [30, D])
    cssn_o = dout("cssn", [64, D])
    csso_o = dout("csso", [16, 26, D])

    P = Plan()

    with nc.sbuf_tensor("arena", [128, ARENA_B // 4], F32) as arena, \
            nc.psum_tensor("ps", [128, 8, 512], F32) as ps:

        def view(off, shape, dt=F32):
            esz = 4 if dt == F32 else 2
            n = int(np.prod(shape))
            a = arena[:, off // 4:(off + n * esz) // 4]
            if dt != F32:
                a = a.bitcast(dt)
            if len(shape) == 2:
                a = a.rearrange("p (a b) -> p a b", a=shape[0])
            elif len(shape) == 3:
                a = a.rearrange("p (a b c) -> p a b c", a=shape[0], b=shape[1])
            return a

        X = view(OFF_X, [KC, NT])
        PV = view(OFF_PV, [NPV])
        IDF = view(OFF_IDF, [128])
        ONES = view(OFF_ONES, [128], BF16)
        EPSV = view(OFF_EPS, [4])
        GTAIL = view(OFF_GTAIL, [KC, 30])
        WST = view(OFF_WST, [16, 128], BF16)
        MS = view(OFF_MS, [16, 64], BF16)
        SMALL = view(OFF_SMALL, [64])
        ST = SMALL[:, 0:24]
        MV = SMALL[:, 24:26]
        RS1 = SMALL[:, 28:29]
        RSTD = view(OFF_RSTD, [1152])
        slots = [view(OFF_SLOTS + i * SLOT_B, [4096], BF16) for i in range(NSLOT)]
        H = view(OFF_H, [KC, NT], BF16)
        HID = view(OFF_HID, [11, NT], BF16)
        HM = view(OFF_HM, [KC, 608], BF16)
        V = view(OFF_V, [5, D], BF16)
        MM_ = view(OFF_V, [KC, 576], BF16)
        US = view(OFF_US, [KC, 576], BF16)
        CN = view(OFF_CN, [KC, 576], BF16)

        banks = Ring([ps[:, b, :] for b in range(8)])
        wring = Ring(slots)

        def mm(out, lhsT, rhs, start, stop, deps=()):
            return P.op("pe", lambda e: e.matmul(out, lhsT, rhs, start=start, stop=stop), deps)

        def tr(out, in_, ident, deps=()):
            return P.op("pe", lambda e: e.transpose(out, in_, ident), deps)

        def act(out, in_, func, deps=(), **kw):
            return P.op("act", lambda e: e.activation(out=out, in_=in_, func=func, **kw), deps)

        def cp(eng, out, in_, deps=()):
            if eng == "act":
                return P.op("act", lambda e: e.copy(out=out, in_=in_), deps)
            return P.op(eng, lambda e: e.tensor_copy(out=out, in_=in_), deps)

        def tt(out, in0, in1, op, deps=(), eng="dve"):
            return P.op(eng, lambda e: e.tensor_tensor(out=out, in0=in0, in1=in1, op=op), deps)

        def ts(out, in0, s1, s2, op0, op1, deps=(), eng="dve"):
            return P.op(eng, lambda e: e.tensor_scalar(out=out, in0=in0, scalar1=s1, scalar2=s2,
                                                       op0=op0, op1=op1), deps)

        def stt(out, in0, scalar, in1, op0, op1, deps=(), eng="dve"):
            return P.op(eng, lambda e: e.scalar_tensor_tensor(out=out, in0=in0, scalar=scalar, in1=in1,
                                                              op0=op0, op1=op1), deps)

        def barrier():
            evs = [P.last(e) for e in ("pe", "act", "dve", "pool")]
            evs = [e for e in evs if e is not None]
            for ok_ in sorted(k for k in P.dcnt if k.startswith("o_")):
                evs.append((ok_, P.dcnt[ok_]))
            for e in ("pe", "act", "dve"):
                for ev in evs:
                    if ev[0] != e:
                        P.wait(e, ev)
            return evs

        w_hold = []

        def wload(parts, extra=()):
            idx, slot, free = wring.get()
            ev = None
            extra = list(extra) + list(w_hold)
            del w_hold[:]
            for i, (dst_fn, src) in enumerate(parts):
                ev = P.dma("pool", dst_fn(slot), src, "w%d" % idx, deps=(free + list(extra)) if i == 0 else ())
            return idx, slot, ev

        def wk(slot, n):
            return slot[:, 0:16 * n].rearrange("p (k n) -> p k n", n=n)

        def colblk(w, c0, n):
            return w[:, c0:c0 + n].rearrange("(k p) n -> p k n", p=128)

        MT = view(OFF_H, [128])
        MSK = view(OFF_H + 512, [64])
        WCOL = view(OFF_H + 1024, [16, 4])
        WSTT = view(OFF_H + 2048, [16, 128])
        c_ev = None
        for dst, src in ((PV, pv_d), (IDF, identf_d), (MT, maskt_d), (MSK[0:64], masks_d),
                         (WCOL[0:64], wcol_d), (WSTT, wst_d)):
            c_ev = P.dma("sp", dst, src, "c0")
        P.op("dve", lambda e: e.memset(ONES, 1.0))
        P.op("dve", lambda e: e.memset(EPSV, EPS))
        tt(WST, WSTT, MT.unsqueeze(1).to_broadcast([128, 16, 128]), ALU.mult, deps=[c_ev])
        tt(MS[0:64].rearrange("p h (i s) -> p h i s", s=16),
           WCOL[0:64].unsqueeze(3).to_broadcast([64, 16, 4, 16]),
           MSK[0:64].rearrange("p (i s) -> p i s", s=16).unsqueeze(1).to_broadcast([64, 16, 4, 16]),
           ALU.mult)
        barrier()

        XT = Ring([view(OFF_HID, [D]), view(OFF_HID + 8192, [D])])
        evq = 0
        for t in range(9):
            rows = 128 if t < 8 else 96
            i, xt, free = XT.get()
            lev = P.dma("sp", xt[0:rows], xin[t * 128:t * 128 + rows, :], "x%d" % i, deps=free)
            if t == 6:
                w_hold.append(lev)
            last_tr = None
            for g in range(4):
                bi, bk, bfree = banks.get()
                for kk in range(4):
                    k = g * 4 + kk
                    last_tr = tr(bk[:, kk * 128:kk * 128 + rows], xt[0:rows, k * 128:(k + 1) * 128],
                                 IDF[0:rows, 0:rows], deps=[lev, c_ev] + bfree)
                src = bk.rearrange("p (a b) -> p a b", b=128)[:, :, 0:rows]
                e = cp("act" if evq % 2 == 0 else "dve", X[:, g * 4:(g + 1) * 4, t * 128:t * 128 + rows], src,
                       deps=[last_tr])
                evq += 1
                banks.release(bi, [e])
            XT.release(i, [last_tr])
        P.dma("sp", csso_o, cache[:, 4:30, :], "out2")
        barrier()

        SQR = Ring([view(OFF_TMPX + 4096, [NT], BF16), view(OFF_TMPX + 4096 + 2240, [NT], BF16)])

        def rms(gidx, segs, Hout, xdeps=()):
            ready = {}
            for (x0, n, l0) in segs:
                bi, bk, bfree = banks.get()
                lastm = None
                for k in range(KC):
                    si, sq, sfree = SQR.get()
                    ea = act(sq[:, 0:n], X[:, k, x0:x0 + n], AF.Square, deps=list(xdeps) + sfree)
                    lastm = mm(bk[:, 0:n], ONES, sq[:, 0:n], k == 0, k == KC - 1,
                               deps=[ea] + (bfree if k == 0 else []))
                    SQR.release(si, [lastm])
                ea = act(RSTD[:, l0:l0 + n], bk[:, 0:n], AF.Sqrt, deps=[lastm], bias=EPSV[:, 0:1], scale=1.0 / D)
                banks.release(bi, [ea])
                er = P.op("dve", lambda e, l0=l0, n=n: e.reciprocal(out=RSTD[:, l0:l0 + n], in_=RSTD[:, l0:l0 + n]),
                          deps=[ea])
                ev = None
                for k in range(KC):
                    ev = stt(Hout[:, k, l0:l0 + n], X[:, k, x0:x0 + n], PV[:, gidx * 16 + k:gidx * 16 + k + 1],
                             RSTD[:, l0:l0 + n], ALU.mult, ALU.mult, deps=[er])
                ready[l0] = ev
            return ready

        FSEGS = [(0, 374, 0), (374, 374, 374), (748, 372, 748)]
        TMPS = Ring([view(OFF_TMPX, [512]), view(OFF_TMPX + 2048, [512])])

        def ffn(gidx, w_gu, w_dn):
            hready = rms(gidx, FSEGS, H)
            hid_free = []
            for gi in range(4):
                hid_ready = None
                for jj in range(11):
                    j = gi * 11 + jj
                    wi, slot, lev = wload([(lambda s: s[:, 0:4096], w_gu[j])])
                    W = wk(slot, 256)
                    lastu = None
                    for (x0, n, l0) in FSEGS:
                        gi_, bg, gfree = banks.get()
                        ui_, bu, ufree = banks.get()
                        lg = None
                        for k in range(KC):
                            lg = mm(bg[:, 0:n], W[:, k, 0:128], H[:, k, x0:x0 + n], k == 0, k == KC - 1,
                                    deps=[lev, hready[l0]] + (gfree if k == 0 else []))
                        for k in range(KC):
                            lastu = mm(bu[:, 0:n], W[:, k, 128:256], H[:, k, x0:x0 + n], k == 0, k == KC - 1,
                                       deps=(ufree if k == 0 else []))
                        ti, tmp, tfree = TMPS.get()
                        ea = act(tmp[:, 0:n], bg[:, 0:n], AF.Silu, deps=[lg] + tfree)
                        banks.release(gi_, [ea])
                        ed = tt(HID[:, jj, x0:x0 + n], tmp[:, 0:n], bu[:, 0:n], ALU.mult,
                                deps=[ea, lastu] + hid_free)
                        banks.release(ui_, [ed])
                        TMPS.release(ti, [ed])
                        hid_ready = ed
                    wring.release(wi, [lastu])
                lastd = None
                for mp in range(8):
                    wi, slot, lev = wload([(lambda s: s[:, 0:2816], w_dn[gi * 8 + mp])])
                    W = slot[:, 0:11 * 256].rearrange("p (j n) -> p j n", n=256)
                    for mi in range(2):
                        m = mp * 2 + mi
                        for (x0, n, l0) in FSEGS:
                            bi, bk, bfree = banks.get()
                            for jj in range(11):
                                lastd = mm(bk[:, 0:n], W[:, jj, mi * 128:(mi + 1) * 128], HID[:, jj, x0:x0 + n],
                                           jj == 0, jj == 10, deps=[lev, hid_ready] + (bfree if jj == 0 else []))
                            ed = stt(X[:, m, x0:x0 + n], bk[:, 0:n], 0.5, X[:, m, x0:x0 + n], ALU.mult, ALU.add,
                                     deps=[lastd])
                            banks.release(bi, [ed])
                    wring.release(wi, [lastd])
                hid_free = [lastd]
            barrier()

        def mixer_pass(p0, samp):
            segs = [(p0, 512, 0)] + ([(1024, 96, 512)] if samp else [])
            bev = barrier()
            hm_ready = rms(PV_MIX, segs, HM)
            hm_all = list(hm_ready.values())
            LNG = view(OFF_CN, [D])
            LNB = view(OFF_CN + 8192, [D])
            ln_ev = P.dma("sp", LNG, lnvg_d.to_broadcast([128, D]), "ln", deps=bev)
            ln_ev = P.dma("sp", LNB, lnvb_d.to_broadcast([128, D]), "ln", deps=bev)
            tiles = [(t, 128, t * 128) for t in range(4)] + ([(4, 64, 544)] if samp else [])
            lastg = None
            gelu_ev = {}
            for nb in range(8):
                wi, slot, lev = wload([(lambda s: s[:, 0:4096], w_v[nb])])
                W = wk(slot, 256)
                lm = None
                for (t, rows, hc) in tiles:
                    bi, bk, bfree = banks.get()
                    for k in range(KC):
                        lm = mm(bk[0:rows, 0:256], HM[:, k, hc:hc + rows], W[:, k, :], k == 0, k == KC - 1,
                                deps=[lev] + hm_all + (bfree if k == 0 else []))
                    lastg = act(V[0:rows, t, nb * 256:(nb + 1) * 256], bk[0:rows, 0:256], AF.Gelu_apprx_tanh,
                                deps=[lm])
                    gelu_ev[t] = lastg
                    banks.release(bi, [lastg])
                wring.release(wi, [lm])
            VTR = Ring([view(OFF_US, [D]), view(OFF_US + 8192, [D])])
            small_free = [[], []]
            v_evs = []
            for ti_, (t, rows, hc) in enumerate(tiles):
                par = ti_ % 2
                STp = SMALL[:, par * 24:(par + 1) * 24]
                MVp = SMALL[:, 48 + par * 2:50 + par * 2]
                RSp = SMALL[:, 52 + par:53 + par]
                NMp = SMALL[:, 54 + par:55 + par]
                e0 = None
                for q in range(4):
                    e0 = P.op("dve", lambda e, q=q, t=t, rows=rows, STp=STp: e.bn_stats(
                        out=STp[0:rows, q * 6:(q + 1) * 6], in_=V[0:rows, t, q * 512:(q + 1) * 512]),
                        deps=[gelu_ev[t]] + small_free[par])
                e1 = P.op("dve", lambda e, rows=rows, STp=STp, MVp=MVp: e.bn_aggr(out=MVp[0:rows, 0:2],
                                                                                  in_=STp[0:rows, 0:24]), deps=[e0])
                e2 = act(RSp[0:rows], MVp[0:rows, 1:2], AF.Sqrt, deps=[e1], bias=EPSV[0:rows, 0:1], scale=1.0)
                e3 = P.op("dve", lambda e, rows=rows, RSp=RSp: e.reciprocal(out=RSp[0:rows], in_=RSp[0:rows]),
                          deps=[e2])
                e3b = stt(NMp[0:rows], MVp[0:rows, 0:1], -1.0, RSp[0:rows], ALU.mult, ALU.mult, deps=[e3])
                is_samp = (t == 4)
                is_out = is_samp or (t == 3 and p0 == 512)
                if is_out:
                    vi, VTt, vfree = VTR.get()
                    e4 = act(VTt[0:rows], V[0:rows, t, :], AF.Identity, deps=[e3b] + vfree,
                             scale=RSp[0:rows, 0:1], bias=NMp[0:rows, 0:1])
                    if not is_samp:
                        e4b = act(V[0:rows, t, :], V[0:rows, t, :], AF.Identity, deps=[e4],
                                  scale=RSp[0:rows, 0:1], bias=NMp[0:rows, 0:1])
                        v_evs.append(e4b)
                        small_free[par] = [e4b]
                    else:
                        small_free[par] = [e4]
                    e5 = tt(VTt[0:rows], VTt[0:rows], LNG[0:rows], ALU.mult, deps=[e4, ln_ev])
                    e6 = tt(VTt[0:rows], VTt[0:rows], LNB[0:rows], ALU.add, deps=[e5])
                    rel = [e6]
                    if is_samp:
                        e7 = cp("dve", V[0:rows, t, :], VTt[0:rows], deps=[e6])
                        v_evs.append(e7)
                        rel.append(e7)
                    od = P.dma("sp", vsamp_o if is_samp else vlast_o, VTt[0:rows], "o_vt%d" % vi, deps=[e6])
                    VTR.release(vi, rel + [od])
                else:
                    e4 = act(V[0:rows, t, :], V[0:rows, t, :], AF.Identity, deps=[e3b],
                             scale=RSp[0:rows, 0:1], bias=NMp[0:rows, 0:1])
                    small_free[par] = [e4]
                    v_evs.append(e4)
            bev = barrier()
            BSB = view(OFF_CN, [D])
            UR = Ring([view(OFF_CN + 8192, [512]), view(OFF_CN + 8192 + 2048, [512])])
            T1R = Ring([view(OFF_CN + 12288, [512]), view(OFF_CN + 12288 + 2048, [512])])
            bs_ev0 = P.dma("sp", BSB, bsr_d.to_broadcast([128, D]), "ln", deps=bev)
            BSS = view(OFF_CN + 16384, [16, 4])
            e_bss = cp("dve", BSS, BSB.rearrange("p (h i) -> p h i", i=128)[:, :, 0:4], deps=[bs_ev0])
            bs_ev = e_bss
            for q in range(4):
                rbi, rbk, rbfree = banks.get()
                erw = mm(rbk[:, 0:512], ONES, WST[:, 4 * q:4 * q + 4, :].rearrange("p a b -> p (a b)"), True, True,
                         deps=rbfree)
                for hh in range(4):
                    h = 4 * q + hh
                    bs_ev = stt(BSB[:, h * 128:(h + 1) * 128], rbk[:, hh * 128:(hh + 1) * 128],
                                PV[:, PV_LVB * 16 + h:PV_LVB * 16 + h + 1], BSB[:, h * 128:(h + 1) * 128],
                                ALU.mult, ALU.add, deps=[erw, e_bss])
                banks.release(rbi, [bs_ev])
            us_last = None
            for cpair in range(8):
                wi, slot, lev = wload([(lambda s: s[:, 0:4096], w_u[cpair])])
                W = wk(slot, 256)
                lm = None
                for ci in range(2):
                    c = cpair * 2 + ci
                    for (x0, n, l0) in segs:
                        bi, bk, bfree = banks.get()
                        for k in range(KC):
                            lm = mm(bk[:, 0:n], W[:, k, ci * 128:(ci + 1) * 128], HM[:, k, l0:l0 + n],
                                    k == 0, k == KC - 1, deps=[lev] + (bfree if k == 0 else []))
                        ui, u, ufree = UR.get()
                        eu = act(u[:, 0:n], bk[:, 0:n], AF.Gelu_apprx_tanh, deps=[lm] + ufree)
                        banks.release(bi, [eu])
                        b2i, b2, b2free = banks.get()
                        t1i, t1, t1free = T1R.get()
                        if n == 512:
                            lg = None
                            for t in range(4):
                                lg = mm(b2[:, t * 128:(t + 1) * 128], V[:, t, c * 128:(c + 1) * 128], WST[:, c, :],
                                        True, True, deps=(b2free if t == 0 else []))
                            e1 = stt(t1[:, 0:512].rearrange("p (t i) -> p t i", i=128),
                                     b2[:, 0:512].rearrange("p (t i) -> p t i", i=128),
                                     PV[:, PV_LVG * 16 + c:PV_LVG * 16 + c + 1],
                                     BSB[:, c * 128:(c + 1) * 128].unsqueeze(1).to_broadcast([128, 4, 128]),
                                     ALU.mult, ALU.add, deps=[lg, bs_ev] + t1free)
                            banks.release(b2i, [e1])
                            us_last = tt(US[:, c, 0:512], t1[:, 0:512], u[:, 0:512], ALU.mult, deps=[e1, eu])
                        else:
                            lg = mm(b2[:, 0:64], V[0:64, 4, c * 128:(c + 1) * 128], MS[0:64, c, :], True, True,
                                    deps=b2free)
                            e1 = tt(t1[:, 0:64].rearrange("p (i s) -> p i s", s=16),
                                    b2[:, 0:64].rearrange("p (i s) -> p i s", s=16),
                                    BSS[:, c, :].unsqueeze(2).to_broadcast([128, 4, 16]),
                                    ALU.add, deps=[lg, bs_ev] + t1free)
                            banks.release(b2i, [e1])
                            us_last = tt(US[:, c, 512:576], t1[:, 0:64], u[:, 32:96], ALU.mult, deps=[e1, eu])
                        UR.release(ui, [us_last])
                        T1R.release(t1i, [us_last])
                wring.release(wi, [lm])
            bev = barrier()
            o = OFF_V
            GB = []
            for _i in range(2):
                GB.append(dict(GPB=view(o, [544], BF16), GSB=view(o + 1088, [16, 34], BF16),
                               GT=view(o + 2176, [96]), GPF=view(o + 2560, [32]), free=[]))
                o += 2688
            TSR = Ring([view(o, [512]), view(o + 2048, [512])]); o += 4096
            CT0 = view(o, [4, 128]); o += 2048
            ACC = CT0.rearrange("p a b -> p (a b)")
            CSTR = Ring([view(o, [128]), view(o + 512, [128])]); o += 1024
            DG0 = view(o, [31, 128], BF16); o += 7936
            assert o <= OFF_V + 20480
            DGR = Ring([DG0, view(OFF_TMPX, [31, 128], BF16)])
            SQC = Ring([view(OFF_TMPX + 7936, [576], BF16), view(OFF_TMPX + 7936 + 1152, [576], BF16)])
            CTR = Ring([CT0, view(OFF_TMPX + 10240, [4, 128])])
            CTR.free = [list(bev), list(bev)]
            GTB = view(OFF_GTAIL, [KC, 30], BF16)
            bstate = {"ct_free": list(bev)}

            def glu_part(c):
                G = GB[c % 2]
                GPB, GSB, GT, GPF, gfree = G["GPB"], G["GSB"], G["GT"], G["GPF"], list(G["free"])
                wi, slot, lev = wload([(lambda s: s[:, 0:4096], w_glu[c])])
                W = wk(slot, 256)
                di, DG, dfree = DGR.get()
                edg = tt(DG, IDF.unsqueeze(1).to_broadcast([128, 31, 128]),
                         PV[:, PV_WDW + c * 31:PV_WDW + (c + 1) * 31].unsqueeze(2).to_broadcast([128, 31, 128]),
                         ALU.mult, deps=dfree)
                egs = None
                if samp:
                    cti, CT, ctfree = CTR.get()
                    cev = P.dma("pool", CT[0:120],
                                cache[:, :, c * 128:(c + 1) * 128].rearrange("(q s) r n -> (s r) q n", q=4),
                                "ct%d" % cti, deps=ctfree)
                    bi, bk, bfree = banks.get()
                    ltr = None
                    for q in range(4):
                        ltr = tr(bk[:, q * 120:(q + 1) * 120], CT[0:120, q, :], IDF[0:120, 0:120],
                                 deps=[cev] + (bfree if q == 0 else []))
                    CTR.release(cti, [ltr])
                    egs = cp("act", GSB[:, :, 0:30], bk[:, 0:480].rearrange("p (s r) -> p s r", r=30),
                             deps=[ltr] + gfree)
                    banks.release(bi, [egs])
                lm = None
                g_evs = []
                for (x0, n, l0) in segs:
                    ai, ba, afree = banks.get()
                    bbi, bb, bbfree = banks.get()
                    la = None
                    for k in range(KC):
                        la = mm(ba[:, 0:n], W[:, k, 0:128], HM[:, k, l0:l0 + n], k == 0, k == KC - 1,
                                deps=[lev] + (afree if k == 0 else []))
                    for k in range(KC):
                        lm = mm(bb[:, 0:n], W[:, k, 128:256], HM[:, k, l0:l0 + n], k == 0, k == KC - 1,
                                deps=(bbfree if k == 0 else []))
                    ti, tsg, tfree = TSR.get()
                    ea = act(tsg[:, 0:n], bb[:, 0:n], AF.Sigmoid, deps=[lm] + tfree)
                    banks.release(bbi, [ea])
                    if n == 512:
                        eg = tt(GPB[:, 30:542], tsg[:, 0:512], ba[:, 0:512], ALU.mult, deps=[ea, la] + gfree)
                        if not samp:
                            g_evs.append(eg)
                            eg = tt(GPF[:, 0:32], tsg[:, 480:512], ba[:, 480:512], ALU.mult, deps=[ea, la] + gfree)
                    else:
                        eg = tt(GT[:, 0:96], tsg[:, 0:96], ba[:, 0:96], ALU.mult, deps=[ea, la] + gfree)
                    banks.release(ai, [eg])
                    TSR.release(ti, [eg])
                    g_evs.append(eg)
                wring.release(wi, [lm])
                if samp:
                    eh = cp("dve", GPB[:, 0:30], GT[:, 2:32], deps=g_evs + gfree)
                    eh2 = cp("dve", GSB[:, :, 30:34].rearrange("p s t -> p t s"),
                             GT[:, 32:96].rearrange("p (t s) -> p t s", s=16), deps=g_evs + [egs] + gfree)
                    eh3 = cp("dve", GTB[:, c, :], GPB[:, 512:542], deps=g_evs)
                    hist = [eh, eh2, eh3]
                else:
                    eh = cp("dve", GPB[:, 0:30], GTB[:, c, :], deps=g_evs + gfree)
                    hist = [eh]
                return dict(G=G, di=di, DG=DG, ready=[edg] + g_evs + hist, g_evs=g_evs)

            def conv_part(c, st):
                G = st["G"]
                GPB, GSB, GT, GPF = G["GPB"], G["GSB"], G["GT"], G["GPF"]
                DG = st["DG"]
                bcol = PV[:, PV_BDW * 16 + c:PV_BDW * 16 + c + 1]
                bci, bc, bcfree = banks.get()
                lc = None
                for k in range(31):
                    lc = mm(bc[:, 0:512], DG[:, k, :], GPB[:, k:k + 512], k == 0, k == 30,
                            deps=st["ready"] + (bcfree if k == 0 else []))
                last = act(CN[:, c, 0:512], bc[:, 0:512], AF.Identity, deps=[lc], bias=bcol, scale=1.0)
                banks.release(bci, [last])
                readers = [lc]
                if samp:
                    bsi, bs2, bsfree = banks.get()
                    ls = None
                    for k in range(31):
                        ls = mm(bs2[:, 0:64].rearrange("p (s t) -> p s t", t=4), DG[:, k, :], GSB[:, :, k:k + 4],
                                k == 0, k == 30, deps=(bsfree if k == 0 else []))
                    last = act(CN[:, c, 512:576].rearrange("p (t s) -> p s t", t=4),
                               bs2[:, 0:64].rearrange("p (s t) -> p s t", t=4), AF.Identity, deps=[ls],
                               bias=bcol, scale=1.0)
                    banks.release(bsi, [last])
                    readers.append(ls)
                DGR.release(st["di"], [readers[-1]])
                bi, bk, bfree = banks.get()
                si, cst, sfree = CSTR.get()
                if samp:
                    etr = tr(bk[0:64, 0:128], GT[:, 32:96], IDF, deps=st["g_evs"] + bfree)
                    ecs = cp("act", cst[0:64], bk[0:64, 0:128], deps=[etr] + sfree)
                    od = P.dma("sp", cssn_o[:, c * 128:(c + 1) * 128], cst[0:64], "o_cs%d" % si, deps=[ecs])
                else:
                    etr = tr(bk[0:32, 0:128], GPF[:, 0:32], IDF, deps=st["g_evs"] + bfree)
                    ecs = cp("act", cst[0:32], bk[0:32, 0:128], deps=[etr] + sfree)
                    od = P.dma("sp", csp_o[:, c * 128:(c + 1) * 128], cst[2:32], "o_cs%d" % si, deps=[ecs])
                banks.release(bi, [ecs])
                CSTR.release(si, [od])
                G["free"] = readers + [etr]
                return last

            cn_last = None
            st_cur = glu_part(0)
            for c in range(KC):
                st_next = glu_part(c + 1) if c + 1 < KC else None
                cn_last = conv_part(c, st_cur)
                st_cur = st_next
            MEAN = view(OFF_RSTD, [576])
            RSC = view(OFF_RSTD + 2304, [576])
            TBR = Ring([view(OFF_V, [576]), view(OFF_V + 2304, [576])])
            gb_dead = GB[0]["free"] + GB[1]["free"] + [cn_last]
            nc_ = 576 if samp else 512
            cseg = [(0, 512)] + ([(512, 64)] if samp else [])
            sbanks = []
            for (l0, n) in cseg:
                si_, bs_, sfree_ = banks.get()
                qi_, bq_, qfree_ = banks.get()
                sbanks.append((si_, bs_, sfree_, qi_, bq_, qfree_))
            lq = None
            for k in range(KC):
                sqi, sq, sqfree = SQC.get()
                ea = act(sq[:, 0:nc_], CN[:, k, 0:nc_], AF.Square, deps=[cn_last] + sqfree)
                for (l0, n), (si_, bs_, sfree_, qi_, bq_, qfree_) in zip(cseg, sbanks):
                    mm(bs_[:, 0:n], ONES, CN[:, k, l0:l0 + n], k == 0, k == KC - 1,
                       deps=[cn_last] + (sfree_ if k == 0 else []))
                    lq = mm(bq_[:, 0:n], ONES, sq[:, l0:l0 + n], k == 0, k == KC - 1,
                            deps=[ea] + (qfree_ if k == 0 else []))
                SQC.release(sqi, [lq])
            e3 = None
            for (l0, n), (si_, bs_, sfree_, qi_, bq_, qfree_) in zip(cseg, sbanks):
                e1 = ts(MEAN[:, l0:l0 + n], bs_[:, 0:n], 1.0 / D, 0.0, ALU.mult, ALU.add, deps=[lq])
                banks.release(si_, [e1])
                e2 = stt(ACC[:, 0:n], MEAN[:, l0:l0 + n], -1.0, MEAN[:, l0:l0 + n], ALU.mult, ALU.mult, deps=[e1])
                e3 = stt(RSC[:, l0:l0 + n], bq_[:, 0:n], 1.0 / D, ACC[:, 0:n], ALU.mult, ALU.add, deps=[e2])
                banks.release(qi_, [e3])
            e4 = act(RSC[:, 0:nc_], RSC[:, 0:nc_], AF.Sqrt, deps=[e3], bias=EPSV[:, 0:1], scale=1.0)
            e5 = P.op("dve", lambda e: e.reciprocal(out=RSC[:, 0:nc_], in_=RSC[:, 0:nc_]), deps=[e4])
            for k in range(KC):
                ti, tb, tfree = TBR.get()
                e6 = tt(tb[:, 0:nc_], CN[:, k, 0:nc_], MEAN[:, 0:nc_], ALU.subtract, deps=[e5] + tfree + gb_dead)
                e7 = tt(tb[:, 0:nc_], tb[:, 0:nc_], RSC[:, 0:nc_], ALU.mult, deps=[e6])
                e8 = act(CN[:, k, 0:nc_], tb[:, 0:nc_], AF.Silu, deps=[e7],
                         scale=PV[:, PV_LCG * 16 + k:PV_LCG * 16 + k + 1],
                         bias=PV[:, PV_LCB * 16 + k:PV_LCB * 16 + k + 1])
                TBR.release(ti, [e8])
            barrier()
            SAR = Ring([view(OFF_TMPX, [512]), view(OFF_TMPX + 2048, [512])])
            SBR = Ring([view(OFF_TMPX + 4096, [512]), view(OFF_TMPX + 6144, [512])])
            TA = view(OFF_TMPX + 8192, [512])
            TB = view(OFF_TMPX + 10240, [512])
            msegs = [(0, 512, 0, p0)] + ([(512, 64, 544, 1056)] if samp else [])
            m_last = None
            for c in range(KC):
                w1i, slot1, lev1 = wload([(lambda s: s[:, 0:4096], w_pab[c])])
                w2i, slot2, lev2 = wload([(lambda s: s[:, 0:4096], w_gate[c])])
                W1 = wk(slot1, 256)
                W2 = wk(slot2, 256)
                l1 = l2 = None
                for (l0, n, hl0, xc) in msegs:
                    yai, bya, f1 = banks.get()
                    ybi, byb, f2 = banks.get()
                    gai, bga, f3 = banks.get()
                    gbi, bgb, f4 = banks.get()
                    lya = lyb = lga = lgb = None
                    for k in range(KC):
                        lya = mm(bya[:, 0:n], W1[:, k, 0:128], US[:, k, l0:l0 + n], k == 0, k == KC - 1,
                                 deps=[lev1] + (f1 if k == 0 else []))
                    for k in range(KC):
                        lyb = mm(byb[:, 0:n], W1[:, k, 128:256], CN[:, k, l0:l0 + n], k == 0, k == KC - 1,
                                 deps=(f2 if k == 0 else []))
                    for k in range(KC):
                        lga = mm(bga[:, 0:n], W2[:, k, 0:128], HM[:, k, hl0:hl0 + n], k == 0, k == KC - 1,
                                 deps=[lev2] + (f3 if k == 0 else []))
                    for k in range(KC):
                        lgb = mm(bgb[:, 0:n], W2[:, k, 128:256], HM[:, k, hl0:hl0 + n], k == 0, k == KC - 1,
                                 deps=(f4 if k == 0 else []))
                    l1, l2 = lyb, lgb
                    sai, sa, saf = SAR.get()
                    sbi, sb, sbf = SBR.get()
                    ea = act(sa[:, 0:n], bga[:, 0:n], AF.Sigmoid, deps=[lga] + saf)
                    eb = act(sb[:, 0:n], bgb[:, 0:n], AF.Sigmoid, deps=[lgb] + sbf)
                    banks.release(gai, [ea])
                    banks.release(gbi, [eb])
                    e1 = tt(TA[:, 0:n], sa[:, 0:n], bya[:, 0:n], ALU.mult, deps=[ea, lya] + ([m_last] if m_last else []))
                    e2 = tt(TB[:, 0:n], sb[:, 0:n], byb[:, 0:n], ALU.mult, deps=[eb, lyb])
                    banks.release(yai, [e1])
                    banks.release(ybi, [e2])
                    m_last = tt(MM_[:, c, l0:l0 + n], TA[:, 0:n], TB[:, 0:n], ALU.add, deps=[e1, e2])
                    SAR.release(sai, [e1])
                    SBR.release(sbi, [e2])
                wring.release(w1i, [l1])
                wring.release(w2i, [l2])
            for mp in range(8):
                wi, slot, lev = wload([(lambda s: s[:, 0:4096], w_out[mp])])
                W = wk(slot, 256)
                lm = None
                for mi in range(2):
                    m = mp * 2 + mi
                    for (l0, n, hl0, xc) in msegs:
                        bi, bk, bfree = banks.get()
                        for k in range(KC):
                            lm = mm(bk[:, 0:n], W[:, k, mi * 128:(mi + 1) * 128], MM_[:, k, l0:l0 + n],
                                    k == 0, k == KC - 1, deps=[lev, m_last] + (bfree if k == 0 else []))
                        ed = tt(X[:, m, xc:xc + n], bk[:, 0:n], X[:, m, xc:xc + n], ALU.add, deps=[lm])
                        banks.release(bi, [ed])
                wring.release(wi, [lm])
            barrier()

        if stop_after != "p0":
            ffn(PV_F1, w_gu1, w_dn1)
        if stop_after not in ("p0", "ffn1"):
            mixer_pass(0, True)
            mixer_pass(512, False)
        if stop_after not in ("p0", "ffn1", "mixer"):
            ffn(PV_F2, w_gu2, w_dn2)

        barrier()
        seg_ready = {}
        if stop_after is None:
            OSEGS = [(0, 512, 0), (512, 512, 512), (1056, 64, 1056)]
            for si_, (x0, n, l0) in enumerate(OSEGS):
                bi, bk, bfree = banks.get()
                lastm = None
                for k in range(KC):
                    si, sq, sfree = SQR.get()
                    ea = act(sq[:, 0:n], X[:, k, x0:x0 + n], AF.Square, deps=sfree)
                    lastm = mm(bk[:, 0:n], ONES, sq[:, 0:n], k == 0, k == KC - 1, deps=[ea] + (bfree if k == 0 else []))
                    SQR.release(si, [lastm])
                ea = act(RSTD[:, l0:l0 + n], bk[:, 0:n], AF.Sqrt, deps=[lastm], bias=EPSV[:, 0:1], scale=1.0 / D)
                banks.release(bi, [ea])
                er = P.op("dve", lambda e, n=n, l0=l0: e.reciprocal(out=RSTD[:, l0:l0 + n], in_=RSTD[:, l0:l0 + n]),
                          deps=[ea])
                er2 = None
                for k in range(KC):
                    er2 = stt(X[:, k, x0:x0 + n], X[:, k, x0:x0 + n],
                              PV[:, PV_FIN * 16 + k:PV_FIN * 16 + k + 1], RSTD[:, l0:l0 + n], ALU.mult, ALU.mult,
                              deps=[er])
                seg_ready[si_] = er2
        YT = Ring([view(OFF_HID, [D]), view(OFF_HID + 8192, [D])])
        evq = 0
        for t in range(9):
            rows = 128 if t < 8 else 64
            c0 = t * 128 if t < 8 else 1056
            sdep = seg_ready.get(0 if t < 4 else (1 if t < 8 else 2))
            sdep = [sdep] if sdep else []
            i, yt, free = YT.get()
            evs = []
            for g in range(4):
                bi, bk, bfree = banks.get()
                ltr = None
                for kk in range(4):
                    k = g * 4 + kk
                    ltr = tr(bk[0:rows, kk * 128:(kk + 1) * 128], X[:, k, c0:c0 + rows], IDF,
                             deps=sdep + (bfree if kk == 0 else []))
                e = cp("act" if evq % 2 == 0 else "dve", yt[0:rows, g * 512:(g + 1) * 512], bk[0:rows, :],
                       deps=[ltr] + free)
                evq += 1
                banks.release(bi, [e])
                evs.append(e)
            od = P.dma("sp", y_o[t * 128:t * 128 + rows, :], yt[0:rows], "o_y%d" % i, deps=evs)
            YT.release(i, [od])
        for ok_ in sorted(k for k in P.dcnt if k.startswith("o_")):
            P.wait("sp", (ok_, P.dcnt[ok_]))
        P.wait("sp", ("out2", P.dcnt["out2"]))

        dkeys = sorted(P.dcnt.keys())
        import contextlib
        with contextlib.ExitStack() as stack:
            esem = {e: stack.enter_context(nc.semaphore("s_" + e)) for e in ("pe", "act", "dve", "pool")}
            dsem = {k: stack.enter_context(nc.semaphore("d_" + k)) for k in dkeys}
            block = stack.enter_context(nc.Block())
            sigval = {}
            for e in ("pe", "act", "dve", "pool"):
                need = sorted(P.needed[e])
                sigval[e] = {idx: i + 1 for i, idx in enumerate(need)}

            def emit(eng, e):
                for item in P.q[eng]:
                    if item[0] == "wait":
                        k, v = item[1], item[2]
                        if k in esem:
                            e.wait_ge(esem[k], sigval[k][v])
                        else:
                            e.wait_ge(dsem[k], v)
                    elif item[0] == "op":
                        ins = item[1](e)
                        idx = item[2]
                        if eng in sigval and idx in sigval[eng]:
                            ins.then_inc(esem[eng], 1)
                    else:
                        ins = e.dma_start(out=item[1], in_=item[2])
                        ins.then_inc(dsem[item[3]], 16)

            @block.tensor
            def _(e):
                emit("pe", e)

            @block.scalar
            def _(e):
                emit("act", e)

            @block.vector
            def _(e):
                emit("dve", e)

            @block.gpsimd
            def _(e):
                emit("pool", e)

            @block.sync
            def _(e):
                emit("sp", e)
    return nc


_CACHE = {}


def _consts():
    identf = np.eye(128, dtype=np.float32)
    jj = np.arange(128)
    maskt = (jj[:, None] <= jj[None, :]).astype(np.float32)
    p = np.arange(64)
    pj, ps_ = p // 16, p % 16
    masks = ((ps_[:, None] == ps_[None, :]) & (pj[:, None] <= pj[None, :])).astype(np.float32)
    return identf, maskt, masks


def _pack_cols(ws, tiles):
    out = np.empty((len(tiles), 128, 4096), np.float32)
    for t, ranges in enumerate(tiles):
        blk = np.concatenate([ws[a][:, c0:c0 + n] for (a, c0, n) in ranges], axis=1)
        out[t] = blk.reshape(16, 128, 256).transpose(1, 0, 2).reshape(128, 4096)
    return out


def _pack_down(w):
    out = np.empty((32, 128, 2816), np.float32)
    for g in range(4):
        for mp in range(8):
            blk = w[g * 1408:(g + 1) * 1408, mp * 256:(mp + 1) * 256]
            out[g * 8 + mp] = blk.reshape(11, 128, 256).transpose(1, 0, 2).reshape(128, 2816)
    return out


def kernel(x_prompt, x_sample, cache_conv, ffn1_norm, ffn1_w_gu, ffn1_w_down, mix_norm,
           w_in, w_s, b_s, ln_v_g, ln_v_b, w_pa, w_dw, b_dw, ln_c_g, ln_c_b, w_pb, w_out,
           ffn2_norm, ffn2_w_gu, ffn2_w_down, final_norm, _stop_after=None):
    f32 = np.float32
    A = lambda a: np.ascontiguousarray(np.asarray(a, dtype=f32))
    x_prompt, x_sample, cache_conv = A(x_prompt), A(x_sample), A(cache_conv)
    key = _stop_after
    if key not in _CACHE:
        _CACHE[key] = build_program(_stop_after)
    nc = _CACHE[key]

    identf, maskt, masks = _consts()
    vecs = [ffn1_norm[0], mix_norm[0], ffn2_norm[0], final_norm, b_dw[0], ln_c_g[0], ln_c_b[0], ln_v_g[0], ln_v_b[0]]
    pv = np.zeros((128, NPV), f32)
    for vi, v in enumerate(vecs):
        pv[:, vi * 16:(vi + 1) * 16] = A(v).reshape(16, 128).T
    wd = A(w_dw[0])
    pv[:, PV_WDW:] = wd.reshape(31, 16, 128).transpose(2, 1, 0).reshape(128, 16 * 31)
    ws = A(w_s[0])
    wst = np.ascontiguousarray(ws.transpose(2, 0, 1))
    wcol = np.ascontiguousarray(np.repeat(ws[:, 0:4, 0:4].transpose(2, 0, 1), 16, axis=0))
    bsr = A(b_s[0]).reshape(1, D)
    lnvg = A(ln_v_g[0]).reshape(1, D)
    lnvb = A(ln_v_b[0]).reshape(1, D)
    win = A(w_in[0])
    shared = {
        "pv": pv, "identf": identf, "maskt": maskt, "masks": masks, "wst": wst, "wcol": wcol,
        "bsr": bsr, "lnvg": lnvg, "lnvb": lnvb,
        "w_gu1": _pack_cols([A(ffn1_w_gu[0])], [[(0, j * 128, 128), (0, DFF + j * 128, 128)] for j in range(44)]),
        "w_dn1": _pack_down(A(ffn1_w_down[0])),
        "w_v": _pack_cols([win], [[(0, D + nb * 256, 256)] for nb in range(8)]),
        "w_u": _pack_cols([win], [[(0, cp_ * 256, 256)] for cp_ in range(8)]),
        "w_glu": _pack_cols([win], [[(0, 2 * D + c * 128, 128), (0, 3 * D + c * 128, 128)] for c in range(16)]),
        "w_gate": _pack_cols([win], [[(0, 4 * D + c * 128, 128), (0, 5 * D + c * 128, 128)] for c in range(16)]),
        "w_pab": _pack_cols([A(w_pa[0]), A(w_pb[0])], [[(0, c * 128, 128), (1, c * 128, 128)] for c in range(16)]),
        "w_out": _pack_cols([A(w_out[0])], [[(0, mp * 256, 256)] for mp in range(8)]),
        "w_gu2": _pack_cols([A(ffn2_w_gu[0])], [[(0, j * 128, 128), (0, DFF + j * 128, 128)] for j in range(44)]),
        "w_dn2": _pack_down(A(ffn2_w_down[0])),
    }
    in_maps = []
    for c in range(8):
        b, hf = c // 2, c % 2
        xc = np.zeros((NT, D), f32)
        xc[0:1024] = x_prompt[b, hf * 1024:(hf + 1) * 1024]
        if hf == 1:
            xc[1024:1056] = x_prompt[b, 992:1024]
        xs = x_sample[c * 16:(c + 1) * 16]
        xc[1056:1120] = xs.transpose(1, 0, 2).reshape(64, D)
        m = dict(shared)
        m["xin"] = xc
        m["cache"] = np.ascontiguousarray(cache_conv[0, c * 16:(c + 1) * 16])
        in_maps.append(m)

    res = run_bass_kernel_spmd(nc, in_maps, core_ids=list(range(8)))
    R = res.results

    y_prompt = np.zeros((4, 2048, D), f32)
    y_sample = np.zeros((128, 4, D), f32)
    cvp = np.zeros((1, 4, 128, D), f32)
    cvs = np.zeros((1, 128, 4, D), f32)
    csp = np.zeros((1, 4, 30, D), f32)
    css = np.zeros((1, 128, 30, D), f32)
    for c in range(8):
        b, hf = c // 2, c % 2
        r = R[c]
        y_prompt[b, hf * 1024:(hf + 1) * 1024] = r["y"][0:1024]
        y_sample[c * 16:(c + 1) * 16] = r["y"][1024:1088].reshape(4, 16, D).transpose(1, 0, 2)
        cvs[0, c * 16:(c + 1) * 16] = r["vsamp"].reshape(4, 16, D).transpose(1, 0, 2)
        css[0, c * 16:(c + 1) * 16, 0:26] = r["csso"]
        css[0, c * 16:(c + 1) * 16, 26:30] = r["cssn"].reshape(4, 16, D).transpose(1, 0, 2)
        if hf == 1:
            cvp[0, b] = r["vlast"]
            csp[0, b] = r["csp"]
    return (y_prompt, y_sample, cvp, cvs, csp, css)
```

```python
import numpy as np
import concourse.bass as bass
import concourse.mybir as mybir
from concourse.bass_utils import run_bass_kernel_spmd

F32 = mybir.dt.float32
BF16 = mybir.dt.bfloat16
AF = mybir.ActivationFunctionType
ALU = mybir.AluOpType

D = 2048
KC = 16
DFF = 5632
NT = 1120
NPV = 9 * 16 + 31 * 16
PV_F1, PV_MIX, PV_F2, PV_FIN, PV_BDW, PV_LCG, PV_LCB, PV_LVG, PV_LVB = range(9)
PV_WDW = 9 * 16
EPS = 1e-6
ENGS = ["pe", "act", "dve", "pool", "sp"]
NSLOT = 4
SLOT_B = 8192

OFF_X = 0
OFF_PV = 71680
OFF_IDF = OFF_PV + 2560
OFF_ONES = OFF_IDF + 512
OFF_EPS = OFF_ONES + 256
OFF_GTAIL = OFF_EPS + 16
OFF_WST = OFF_GTAIL + 1920
OFF_MS = OFF_WST + 4096
OFF_SMALL = OFF_MS + 2048
PL0 = 83456
assert OFF_SMALL + 256 <= PL0
OFF_RSTD = PL0
OFF_SLOTS = OFF_RSTD + 4608
OFF_TMPX = OFF_SLOTS + NSLOT * SLOT_B
REG0 = OFF_TMPX + 12288
OFF_H = REG0
OFF_HID = REG0 + 35840
OFF_HM = REG0
OFF_V = REG0 + 19456
OFF_US = OFF_V + 20480
OFF_CN = OFF_US + 18432
ARENA_B = REG0 + 78848


class Plan:
    def __init__(self):
        self.q = {e: [] for e in ENGS}
        self.n = {e: 0 for e in ENGS}
        self.waited = {e: {} for e in ENGS}
        self.dcnt = {}
        self.needed = {e: set() for e in ENGS}

    def wait(self, eng, ev):
        if ev is None:
            return
        k, v = ev
        if self.waited[eng].get(k, 0) >= v:
            return
        self.waited[eng][k] = v
        if k in self.needed:
            self.needed[k].add(v)
        self.q[eng].append(("wait", k, v))

    def op(self, eng, fn, deps=()):
        for d in deps:
            self.wait(eng, d)
        self.n[eng] += 1
        self.q[eng].append(("op", fn, self.n[eng]))
        return (eng, self.n[eng])

    def dma(self, eng, out, in_, semkey, deps=()):
        for d in deps:
            self.wait(eng, d)
        c = self.dcnt.get(semkey, 0) + 16
        self.dcnt[semkey] = c
        self.q[eng].append(("dma", out, in_, semkey))
        return (semkey, c)

    def last(self, eng):
        return (eng, self.n[eng]) if self.n[eng] > 0 else None


class Ring:
    def __init__(self, items):
        self.items = items
        self.i = 0
        self.free = [[] for _ in items]

    def get(self):
        idx = self.i % len(self.items)
        self.i += 1
        return idx, self.items[idx], list(self.free[idx])

    def release(self, idx, events):
        self.free[idx] = [e for e in events if e is not None]


def build_program(stop_after=None):
    nc = bass.Bass("TRN2", target_bir_lowering=False)

    def din(name, shape):
        return nc.dram_tensor(name, shape, F32, kind="ExternalInput").ap()

    def dout(name, shape):
        return nc.dram_tensor(name, shape, F32, kind="ExternalOutput").ap()

    xin = din("xin", [NT, D])
    cache = din("cache", [16, 30, D])
    pv_d = din("pv", [128, NPV])
    identf_d = din("identf", [128, 128])
    maskt_d = din("maskt", [128, 128])
    masks_d = din("masks", [64, 64])
    wst_d = din("wst", [128, 16, 128])
    wcol_d = din("wcol", [64, 16, 4])
    bsr_d = din("bsr", [1, D])
    lnvg_d = din("lnvg", [1, D])
    lnvb_d = din("lnvb", [1, D])
    w_gu1 = din("w_gu1", [44, 128, 4096])
    w_dn1 = din("w_dn1", [32, 128, 2816])
    w_v = din("w_v", [8, 128, 4096])
    w_u = din("w_u", [8, 128, 4096])
    w_glu = din("w_glu", [16, 128, 4096])
    w_gate = din("w_gate", [16, 128, 4096])
    w_pab = din("w_pab", [16, 128, 4096])
    w_out = din("w_out", [8, 128, 4096])
    w_gu2 = din("w_gu2", [44, 128, 4096])
    w_dn2 = din("w_dn2", [32, 128, 2816])

    y_o = dout("y", [1088, D])
    vlast_o = dout("vlast", [128, D])
    vsamp_o = dout("vsamp", [64, D])
    csp_o = dout("csp", [30, D])
    cssn_o = dout("cssn", [64, D])
    csso_o = dout("csso", [16, 26, D])

    P = Plan()

    with nc.sbuf_tensor("arena", [128, ARENA_B // 4], F32) as arena, \
            nc.psum_tensor("ps", [128, 8, 512], F32) as ps:

        def view(off, shape, dt=F32):
            esz = 4 if dt == F32 else 2
            n = int(np.prod(shape))
            a = arena[:, off // 4:(off + n * esz) // 4]
            if dt != F32:
                a = a.bitcast(dt)
            if len(shape) == 2:
                a = a.rearrange("p (a b) -> p a b", a=shape[0])
            elif len(shape) == 3:
                a = a.rearrange("p (a b c) -> p a b c", a=shape[0], b=shape[1])
            return a

        X = view(OFF_X, [KC, NT])
        PV = view(OFF_PV, [NPV])
        IDF = view(OFF_IDF, [128])
        ONES = view(OFF_ONES, [128], BF16)
        EPSV = view(OFF_EPS, [4])
        GTAIL = view(OFF_GTAIL, [KC, 30])
        WST = view(OFF_WST, [16, 128], BF16)
        MS = view(OFF_MS, [16, 64], BF16)
        SMALL = view(OFF_SMALL, [64])
        ST = SMALL[:, 0:24]
        MV = SMALL[:, 24:26]
        RS1 = SMALL[:, 28:29]
        RSTD = view(OFF_RSTD, [1152])
        slots = [view(OFF_SLOTS + i * SLOT_B, [4096], BF16) for i in range(NSLOT)]
        H = view(OFF_H, [KC, NT], BF16)
        HID = view(OFF_HID, [11, NT], BF16)
        HM = view(OFF_HM, [KC, 608], BF16)
        V = view(OFF_V, [5, D], BF16)
        MM_ = view(OFF_V, [KC, 576], BF16)
        US = view(OFF_US, [KC, 576], BF16)
        CN = view(OFF_CN, [KC, 576], BF16)

        banks = Ring([ps[:, b, :] for b in range(8)])
        wring = Ring(slots)

        def mm(out, lhsT, rhs, start, stop, deps=()):
            return P.op("pe", lambda e: e.matmul(out, lhsT, rhs, start=start, stop=stop), deps)

        def tr(out, in_, ident, deps=()):
            return P.op("pe", lambda e: e.transpose(out, in_, ident), deps)

        def act(out, in_, func, deps=(), **kw):
            return P.op("act", lambda e: e.activation(out=out, in_=in_, func=func, **kw), deps)

        def cp(eng, out, in_, deps=()):
            if eng == "act":
                return P.op("act", lambda e: e.copy(out=out, in_=in_), deps)
            return P.op(eng, lambda e: e.tensor_copy(out=out, in_=in_), deps)

        def tt(out, in0, in1, op, deps=(), eng="dve"):
            return P.op(eng, lambda e: e.tensor_tensor(out=out, in0=in0, in1=in1, op=op), deps)

        def ts(out, in0, s1, s2, op0, op1, deps=(), eng="dve"):
            return P.op(eng, lambda e: e.tensor_scalar(out=out, in0=in0, scalar1=s1, scalar2=s2,
                                                       op0=op0, op1=op1), deps)

        def stt(out, in0, scalar, in1, op0, op1, deps=(), eng="dve"):
            return P.op(eng, lambda e: e.scalar_tensor_tensor(out=out, in0=in0, scalar=scalar, in1=in1,
                                                              op0=op0, op1=op1), deps)

        def barrier():
            evs = [P.last(e) for e in ("pe", "act", "dve", "pool")]
            evs = [e for e in evs if e is not None]
            for ok_ in sorted(k for k in P.dcnt if k.startswith("o_")):
                evs.append((ok_, P.dcnt[ok_]))
            for e in ("pe", "act", "dve"):
                for ev in evs:
                    if ev[0] != e:
                        P.wait(e, ev)
            return evs

        w_hold = []

        def wload(parts, extra=()):
            idx, slot, free = wring.get()
            ev = None
            extra = list(extra) + list(w_hold)
            del w_hold[:]
            for i, (dst_fn, src) in enumerate(parts):
                ev = P.dma("pool", dst_fn(slot), src, "w%d" % idx, deps=(free + list(extra)) if i == 0 else ())
            return idx, slot, ev

        def wk(slot, n):
            return slot[:, 0:16 * n].rearrange("p (k n) -> p k n", n=n)

        def colblk(w, c0, n):
            return w[:, c0:c0 + n].rearrange("(k p) n -> p k n", p=128)

        MT = view(OFF_H, [128])
        MSK = view(OFF_H + 512, [64])
        WCOL = view(OFF_H + 1024, [16, 4])
        WSTT = view(OFF_H + 2048, [16, 128])
        c_ev = None
        for dst, src in ((PV, pv_d), (IDF, identf_d), (MT, maskt_d), (MSK[0:64], masks_d),
                         (WCOL[0:64], wcol_d), (WSTT, wst_d)):
            c_ev = P.dma("sp", dst, src, "c0")
        P.op("dve", lambda e: e.memset(ONES, 1.0))
        P.op("dve", lambda e: e.memset(EPSV, EPS))
        tt(WST, WSTT, MT.unsqueeze(1).to_broadcast([128, 16, 128]), ALU.mult, deps=[c_ev])
        tt(MS[0:64].rearrange("p h (i s) -> p h i s", s=16),
           WCOL[0:64].unsqueeze(3).to_broadcast([64, 16, 4, 16]),
           MSK[0:64].rearrange("p (i s) -> p i s", s=16).unsqueeze(1).to_broadcast([64, 16, 4, 16]),
           ALU.mult)
        barrier()

        XT = Ring([view(OFF_HID, [D]), view(OFF_HID + 8192, [D]), view(OFF_HID + 16384, [D])])
        evq = 0
        for t in range(9):
            rows = 128 if t < 8 else 96
            i, xt, free = XT.get()
            lev = P.dma("sp", xt[0:rows], xin[t * 128:t * 128 + rows, :], "x%d" % i, deps=free)
            if t == 6:
                w_hold.append(lev)
            last_tr = None
            for g in range(4):
                bi, bk, bfree = banks.get()
                for kk in range(4):
                    k = g * 4 + kk
                    last_tr = tr(bk[:, kk * 128:kk * 128 + rows], xt[0:rows, k * 128:(k + 1) * 128],
                                 IDF[0:rows, 0:rows], deps=[lev, c_ev] + bfree)
                src = bk.rearrange("p (a b) -> p a b", b=128)[:, :, 0:rows]
                e = cp("act" if evq % 2 == 0 else "dve", X[:, g * 4:(g + 1) * 4, t * 128:t * 128 + rows], src,
                       deps=[last_tr])
                evq += 1
                banks.release(bi, [e])
            XT.release(i, [last_tr])
        P.dma("sp", csso_o, cache[:, 4:30, :], "out2")
        barrier()

        SQR = Ring([view(OFF_TMPX + 4096, [NT], BF16), view(OFF_TMPX + 4096 + 2240, [NT], BF16)])

        def rms(gidx, segs, Hout, xdeps=()):
            ready = {}
            for (x0, n, l0) in segs:
                bi, bk, bfree = banks.get()
                lastm = None
                for k in range(KC):
                    si, sq, sfree = SQR.get()
                    ea = act(sq[:, 0:n], X[:, k, x0:x0 + n], AF.Square, deps=list(xdeps) + sfree)
                    lastm = mm(bk[:, 0:n], ONES, sq[:, 0:n], k == 0, k == KC - 1,
                               deps=[ea] + (bfree if k == 0 else []))
                    SQR.release(si, [lastm])
                ea = act(RSTD[:, l0:l0 + n], bk[:, 0:n], AF.Sqrt, deps=[lastm], bias=EPSV[:, 0:1], scale=1.0 / D)
                banks.release(bi, [ea])
                er = P.op("dve", lambda e, l0=l0, n=n: e.reciprocal(out=RSTD[:, l0:l0 + n], in_=RSTD[:, l0:l0 + n]),
                          deps=[ea])
                ev = None
                for k in range(KC):
                    ev = stt(Hout[:, k, l0:l0 + n], X[:, k, x0:x0 + n], PV[:, gidx * 16 + k:gidx * 16 + k + 1],
                             RSTD[:, l0:l0 + n], ALU.mult, ALU.mult, deps=[er])
                ready[l0] = ev
            return ready

        FSEGS = [(0, 374, 0), (374, 374, 374), (748, 372, 748)]
        TMPS = Ring([view(OFF_TMPX, [512]), view(OFF_TMPX + 2048, [512])])

        def ffn(gidx, w_gu, w_dn):
            hready = rms(gidx, FSEGS, H)
            hid_free = []
            for gi in range(4):
                hid_ready = None
                for jj in range(11):
                    j = gi * 11 + jj
                    wi, slot, lev = wload([(lambda s: s[:, 0:4096], w_gu[j])])
                    W = wk(slot, 256)
                    lastu = None
                    for (x0, n, l0) in FSEGS:
                        gi_, bg, gfree = banks.get()
                        ui_, bu, ufree = banks.get()
                        lg = None
                        for k in range(KC):
                            lg = mm(bg[:, 0:n], W[:, k, 0:128], H[:, k, x0:x0 + n], k == 0, k == KC - 1,
                                    deps=[lev, hready[l0]] + (gfree if k == 0 else []))
                        for k in range(KC):
                            lastu = mm(bu[:, 0:n], W[:, k, 128:256], H[:, k, x0:x0 + n], k == 0, k == KC - 1,
                                       deps=(ufree if k == 0 else []))
                        ti, tmp, tfree = TMPS.get()
                        ea = act(tmp[:, 0:n], bg[:, 0:n], AF.Silu, deps=[lg] + tfree)
                        banks.release(gi_, [ea])
                        ed = tt(HID[:, jj, x0:x0 + n], tmp[:, 0:n], bu[:, 0:n], ALU.mult,
                                deps=[ea, lastu] + hid_free)
                        banks.release(ui_, [ed])
                        TMPS.release(ti, [ed])
                        hid_ready = ed
                    wring.release(wi, [lastu])
                lastd = None
                for mp in range(8):
                    wi, slot, lev = wload([(lambda s: s[:, 0:2816], w_dn[gi * 8 + mp])])
                    W = slot[:, 0:11 * 256].rearrange("p (j n) -> p j n", n=256)
                    for mi in range(2):
                        m = mp * 2 + mi
                        for (x0, n, l0) in FSEGS:
                            bi, bk, bfree = banks.get()
                            for jj in range(11):
                                lastd = mm(bk[:, 0:n], W[:, jj, mi * 128:(mi + 1) * 128], HID[:, jj, x0:x0 + n],
                                           jj == 0, jj == 10, deps=[lev, hid_ready] + (bfree if jj == 0 else []))
                            ed = stt(X[:, m, x0:x0 + n], bk[:, 0:n], 0.5, X[:, m, x0:x0 + n], ALU.mult, ALU.add,
                                     deps=[lastd])
                            banks.release(bi, [ed])
                    wring.release(wi, [lastd])
                hid_free = [lastd]
            barrier()

        def mixer_pass(p0, samp):
            segs = [(p0, 512, 0)] + ([(1024, 96, 512)] if samp else [])
            bev = barrier()
            hm_ready = rms(PV_MIX, segs, HM)
            hm_all = list(hm_ready.values())
            LNG = view(OFF_CN, [D])
            LNB = view(OFF_CN + 8192, [D])
            ln_ev = P.dma("sp", LNG, lnvg_d.to_broadcast([128, D]), "ln", deps=bev)
            ln_ev = P.dma("sp", LNB, lnvb_d.to_broadcast([128, D]), "ln", deps=bev)
            tiles = [(t, 128, t * 128) for t in range(4)] + ([(4, 64, 544)] if samp else [])
            lastg = None
            gelu_ev = {}
            for nb in range(8):
                wi, slot, lev = wload([(lambda s: s[:, 0:4096], w_v[nb])])
                W = wk(slot, 256)
                lm = None
                for (t, rows, hc) in tiles:
                    bi, bk, bfree = banks.get()
                    for k in range(KC):
                        lm = mm(bk[0:rows, 0:256], HM[:, k, hc:hc + rows], W[:, k, :], k == 0, k == KC - 1,
                                deps=[lev] + hm_all + (bfree if k == 0 else []))
                    lastg = act(V[0:rows, t, nb * 256:(nb + 1) * 256], bk[0:rows, 0:256], AF.Gelu_apprx_tanh,
                                deps=[lm])
                    gelu_ev[t] = lastg
                    banks.release(bi, [lastg])
                wring.release(wi, [lm])
            VTR = Ring([view(OFF_US, [D]), view(OFF_US + 8192, [D])])
            small_free = [[], []]
            v_evs = []
            for ti_, (t, rows, hc) in enumerate(tiles):
                par = ti_ % 2
                STp = SMALL[:, par * 24:(par + 1) * 24]
                MVp = SMALL[:, 48 + par * 2:50 + par * 2]
                RSp = SMALL[:, 52 + par:53 + par]
                NMp = SMALL[:, 54 + par:55 + par]
                e0 = None
                for q in range(4):
                    e0 = P.op("dve", lambda e, q=q, t=t, rows=rows, STp=STp: e.bn_stats(
                        out=STp[0:rows, q * 6:(q + 1) * 6], in_=V[0:rows, t, q * 512:(q + 1) * 512]),
                        deps=[gelu_ev[t]] + small_free[par])
                e1 = P.op("dve", lambda e, rows=rows, STp=STp, MVp=MVp: e.bn_aggr(out=MVp[0:rows, 0:2],
                                                                                  in_=STp[0:rows, 0:24]), deps=[e0])
                e2 = act(RSp[0:rows], MVp[0:rows, 1:2], AF.Sqrt, deps=[e1], bias=EPSV[0:rows, 0:1], scale=1.0)
                e3 = P.op("dve", lambda e, rows=rows, RSp=RSp: e.reciprocal(out=RSp[0:rows], in_=RSp[0:rows]),
                          deps=[e2])
                e3b = stt(NMp[0:rows], MVp[0:rows, 0:1], -1.0, RSp[0:rows], ALU.mult, ALU.mult, deps=[e3])
                is_samp = (t == 4)
                is_out = is_samp or (t == 3 and p0 == 512)
                if is_out:
                    vi, VTt, vfree = VTR.get()
                    e4 = act(VTt[0:rows], V[0:rows, t, :], AF.Identity, deps=[e3b] + vfree,
                             scale=RSp[0:rows, 0:1], bias=NMp[0:rows, 0:1])
                    if not is_samp:
                        e4b = act(V[0:rows, t, :], V[0:rows, t, :], AF.Identity, deps=[e4],
                                  scale=RSp[0:rows, 0:1], bias=NMp[0:rows, 0:1])
                        v_evs.append(e4b)
                        small_free[par] = [e4b]
                    else:
                        small_free[par] = [e4]
                    e5 = tt(VTt[0:rows], VTt[0:rows], LNG[0:rows], ALU.mult, deps=[e4, ln_ev])
                    e6 = tt(VTt[0:rows], VTt[0:rows], LNB[0:rows], ALU.add, deps=[e5])
                    rel = [e6]
                    if is_samp:
                        e7 = cp("dve", V[0:rows, t, :], VTt[0:rows], deps=[e6])
                        v_evs.append(e7)
                        rel.append(e7)
                    od = P.dma("sp", vsamp_o if is_samp else vlast_o, VTt[0:rows], "o_vt%d" % vi, deps=[e6])
                    VTR.release(vi, rel + [od])
                else:
                    e4 = act(V[0:rows, t, :], V[0:rows, t, :], AF.Identity, deps=[e3b],
                             scale=RSp[0:rows, 0:1], bias=NMp[0:rows, 0:1])
                    small_free[par] = [e4]
                    v_evs.append(e4)
            bev = barrier()
            BSB = view(OFF_CN, [D])
            UR = Ring([view(OFF_CN + 8192, [512]), view(OFF_CN + 8192 + 2048, [512])])
            T1R = Ring([view(OFF_CN + 12288, [512]), view(OFF_CN + 12288 + 2048, [512])])
            bs_ev0 = P.dma("sp", BSB, bsr_d.to_broadcast([128, D]), "ln", deps=bev)
            BSS = view(OFF_CN + 16384, [16, 4])
            e_bss = cp("dve", BSS, BSB.rearrange("p (h i) -> p h i", i=128)[:, :, 0:4], deps=[bs_ev0])
            bs_ev = e_bss
            for q in range(4):
                rbi, rbk, rbfree = banks.get()
                erw = mm(rbk[:, 0:512], ONES, WST[:, 4 * q:4 * q + 4, :].rearrange("p a b -> p (a b)"), True, True,
                         deps=rbfree)
                for hh in range(4):
                    h = 4 * q + hh
                    bs_ev = stt(BSB[:, h * 128:(h + 1) * 128], rbk[:, hh * 128:(hh + 1) * 128],
                                PV[:, PV_LVB * 16 + h:PV_LVB * 16 + h + 1], BSB[:, h * 128:(h + 1) * 128],
                                ALU.mult, ALU.add, deps=[erw, e_bss])
                banks.release(rbi, [bs_ev])
            us_last = None
            for cpair in range(8):
                wi, slot, lev = wload([(lambda s: s[:, 0:4096], w_u[cpair])])
                W = wk(slot, 256)
                lm = None
                for ci in range(2):
                    c = cpair * 2 + ci
                    for (x0, n, l0) in segs:
                        bi, bk, bfree = banks.get()
                        for k in range(KC):
                            lm = mm(bk[:, 0:n], W[:, k, ci * 128:(ci + 1) * 128], HM[:, k, l0:l0 + n],
                                    k == 0, k == KC - 1, deps=[lev] + (bfree if k == 0 else []))
                        ui, u, ufree = UR.get()
                        eu = act(u[:, 0:n], bk[:, 0:n], AF.Gelu_apprx_tanh, deps=[lm] + ufree)
                        banks.release(bi, [eu])
                        b2i, b2, b2free = banks.get()
                        t1i, t1, t1free = T1R.get()
                        if n == 512:
                            lg = None
                            for t in range(4):
                                lg = mm(b2[:, t * 128:(t + 1) * 128], V[:, t, c * 128:(c + 1) * 128], WST[:, c, :],
                                        True, True, deps=(b2free if t == 0 else []))
                            e1 = stt(t1[:, 0:512].rearrange("p (t i) -> p t i", i=128),
                                     b2[:, 0:512].rearrange("p (t i) -> p t i", i=128),
                                     PV[:, PV_LVG * 16 + c:PV_LVG * 16 + c + 1],
                                     BSB[:, c * 128:(c + 1) * 128].unsqueeze(1).to_broadcast([128, 4, 128]),
                                     ALU.mult, ALU.add, deps=[lg, bs_ev] + t1free)
                            banks.release(b2i, [e1])
                            us_last = tt(US[:, c, 0:512], t1[:, 0:512], u[:, 0:512], ALU.mult, deps=[e1, eu])
                        else:
                            lg = mm(b2[:, 0:64], V[0:64, 4, c * 128:(c + 1) * 128], MS[0:64, c, :], True, True,
                                    deps=b2free)
                            e1 = tt(t1[:, 0:64].rearrange("p (i s) -> p i s", s=16),
                                    b2[:, 0:64].rearrange("p (i s) -> p i s", s=16),
                                    BSS[:, c, :].unsqueeze(2).to_broadcast([128, 4, 16]),
                                    ALU.add, deps=[lg, bs_ev] + t1free)
                            banks.release(b2i, [e1])
                            us_last = tt(US[:, c, 512:576], t1[:, 0:64], u[:, 32:96], ALU.mult, deps=[e1, eu])
                        UR.release(ui, [us_last])
                        T1R.release(t1i, [us_last])
                wring.release(wi, [lm])
            bev = barrier()
            o = OFF_V
            GB = []
            for _i in range(2):
                GB.append(dict(GPB=view(o, [544], BF16), GSB=view(o + 1088, [16, 34], BF16),
                               GT=view(o + 2176, [96]), GPF=view(o + 2560, [32]), free=[]))
                o += 2688
            TSR = Ring([view(o, [512]), view(o + 2048, [512])]); o += 4096
            CT0 = view(o, [4, 128]); o += 2048
            ACC = CT0.rearrange("p a b -> p (a b)")
            CSTR = Ring([view(o, [128]), view(o + 512, [128])]); o += 1024
            DG0 = view(o, [31, 128], BF16); o += 7936
            assert o <= OFF_V + 20480
            DGR = Ring([DG0, view(OFF_TMPX, [31, 128], BF16)])
            SQC = Ring([view(OFF_TMPX + 7936, [576], BF16), view(OFF_TMPX + 7936 + 1152, [576], BF16)])
            CTR = Ring([CT0, view(OFF_TMPX + 10240, [4, 128])])
            CTR.free = [list(bev), list(bev)]
            GTB = view(OFF_GTAIL, [KC, 30], BF16)
            bstate = {"ct_free": list(bev)}

            def glu_part(c):
                G = GB[c % 2]
                GPB, GSB, GT, GPF, gfree = G["GPB"], G["GSB"], G["GT"], G["GPF"], list(G["free"])
                wi, slot, lev = wload([(lambda s: s[:, 0:4096], w_glu[c])])
                W = wk(slot, 256)
                di, DG, dfree = DGR.get()
                edg = tt(DG, IDF.unsqueeze(1).to_broadcast([128, 31, 128]),
                         PV[:, PV_WDW + c * 31:PV_WDW + (c + 1) * 31].unsqueeze(2).to_broadcast([128, 31, 128]),
                         ALU.mult, deps=dfree)
                egs = None
                if samp:
                    cti, CT, ctfree = CTR.get()
                    cev = P.dma("pool", CT[0:120],
                                cache[:, :, c * 128:(c + 1) * 128].rearrange("(q s) r n -> (s r) q n", q=4),
                                "ct%d" % cti, deps=ctfree)
                    bi, bk, bfree = banks.get()
                    ltr = None
                    for q in range(4):
                        ltr = tr(bk[:, q * 120:(q + 1) * 120], CT[0:120, q, :], IDF[0:120, 0:120],
                                 deps=[cev] + (bfree if q == 0 else []))
                    CTR.release(cti, [ltr])
                    egs = cp("act", GSB[:, :, 0:30], bk[:, 0:480].rearrange("p (s r) -> p s r", r=30),
                             deps=[ltr] + gfree)
                    banks.release(bi, [egs])
                lm = None
                g_evs = []
                for (x0, n, l0) in segs:
                    ai, ba, afree = banks.get()
                    bbi, bb, bbfree = banks.get()
                    la = None
                    for k in range(KC):
                        la = mm(ba[:, 0:n], W[:, k, 0:128], HM[:, k, l0:l0 + n], k == 0, k == KC - 1,
                                deps=[lev] + (afree if k == 0 else []))
                    for k in range(KC):
                        lm = mm(bb[:, 0:n], W[:, k, 128:256], HM[:, k, l0:l0 + n], k == 0, k == KC - 1,
                                deps=(bbfree if k == 0 else []))
                    ti, tsg, tfree = TSR.get()
                    ea = act(tsg[:, 0:n], bb[:, 0:n], AF.Sigmoid, deps=[lm] + tfree)
                    banks.release(bbi, [ea])
                    if n == 512:
                        eg = tt(GPB[:, 30:542], tsg[:, 0:512], ba[:, 0:512], ALU.mult, deps=[ea, la] + gfree)
                        if not samp:
                            g_evs.append(eg)
                            eg = tt(GPF[:, 0:32], tsg[:, 480:512], ba[:, 480:512], ALU.mult, deps=[ea, la] + gfree)
                    else:
                        eg = tt(GT[:, 0:96], tsg[:, 0:96], ba[:, 0:96], ALU.mult, deps=[ea, la] + gfree)
                    banks.release(ai, [eg])
                    TSR.release(ti, [eg])
                    g_evs.append(eg)
                wring.release(wi, [lm])
                if samp:
                    eh = cp("dve", GPB[:, 0:30], GT[:, 2:32], deps=g_evs + gfree)
                    eh2 = cp("dve", GSB[:, :, 30:34].rearrange("p s t -> p t s"),
                             GT[:, 32:96].rearrange("p (t s) -> p t s", s=16), deps=g_evs + [egs] + gfree)
                    eh3 = cp("dve", GTB[:, c, :], GPB[:, 512:542], deps=g_evs)
                    hist = [eh, eh2, eh3]
                else:
                    eh = cp("dve", GPB[:, 0:30], GTB[:, c, :], deps=g_evs + gfree)
                    hist = [eh]
                return dict(G=G, di=di, DG=DG, ready=[edg] + g_evs + hist, g_evs=g_evs)

            def conv_part(c, st):
                G = st["G"]
                GPB, GSB, GT, GPF = G["GPB"], G["GSB"], G["GT"], G["GPF"]
                DG = st["DG"]
                bcol = PV[:, PV_BDW * 16 + c:PV_BDW * 16 + c + 1]
                bci, bc, bcfree = banks.get()
                lc = None
                for k in range(31):
                    lc = mm(bc[:, 0:512], DG[:, k, :], GPB[:, k:k + 512], k == 0, k == 30,
                            deps=st["ready"] + (bcfree if k == 0 else []))
                last = act(CN[:, c, 0:512], bc[:, 0:512], AF.Identity, deps=[lc], bias=bcol, scale=1.0)
                banks.release(bci, [last])
                readers = [lc]
                if samp:
                    bsi, bs2, bsfree = banks.get()
                    ls = None
                    for k in range(31):
                        ls = mm(bs2[:, 0:64].rearrange("p (s t) -> p s t", t=4), DG[:, k, :], GSB[:, :, k:k + 4],
                                k == 0, k == 30, deps=(bsfree if k == 0 else []))
                    last = act(CN[:, c, 512:576].rearrange("p (t s) -> p s t", t=4),
                               bs2[:, 0:64].rearrange("p (s t) -> p s t", t=4), AF.Identity, deps=[ls],
                               bias=bcol, scale=1.0)
                    banks.release(bsi, [last])
                    readers.append(ls)
                DGR.release(st["di"], [readers[-1]])
                bi, bk, bfree = banks.get()
                si, cst, sfree = CSTR.get()
                if samp:
                    etr = tr(bk[0:64, 0:128], GT[:, 32:96], IDF, deps=st["g_evs"] + bfree)
                    ecs = cp("act", cst[0:64], bk[0:64, 0:128], deps=[etr] + sfree)
                    od = P.dma("sp", cssn_o[:, c * 128:(c + 1) * 128], cst[0:64], "o_cs%d" % si, deps=[ecs])
                else:
                    etr = tr(bk[0:32, 0:128], GPF[:, 0:32], IDF, deps=st["g_evs"] + bfree)
                    ecs = cp("act", cst[0:32], bk[0:32, 0:128], deps=[etr] + sfree)
                    od = P.dma("sp", csp_o[:, c * 128:(c + 1) * 128], cst[2:32], "o_cs%d" % si, deps=[ecs])
                banks.release(bi, [ecs])
                CSTR.release(si, [od])
                G["free"] = readers + [etr]
                return last

            cn_last = None
            st_cur = glu_part(0)
            for c in range(KC):
                st_next = glu_part(c + 1) if c + 1 < KC else None
                cn_last = conv_part(c, st_cur)
                st_cur = st_next
            MEAN = view(OFF_RSTD, [576])
            RSC = view(OFF_RSTD + 2304, [576])
            TBR = Ring([view(OFF_V, [576]), view(OFF_V + 2304, [576])])
            gb_dead = GB[0]["free"] + GB[1]["free"] + [cn_last]
            nc_ = 576 if samp else 512
            cseg = [(0, 512)] + ([(512, 64)] if samp else [])
            sbanks = []
            for (l0, n) in cseg:
                si_, bs_, sfree_ = banks.get()
                qi_, bq_, qfree_ = banks.get()
                sbanks.append((si_, bs_, sfree_, qi_, bq_, qfree_))
            lq = None
            for k in range(KC):
                sqi, sq, sqfree = SQC.get()
                ea = act(sq[:, 0:nc_], CN[:, k, 0:nc_], AF.Square, deps=[cn_last] + sqfree)
                for (l0, n), (si_, bs_, sfree_, qi_, bq_, qfree_) in zip(cseg, sbanks):
                    mm(bs_[:, 0:n], ONES, CN[:, k, l0:l0 + n], k == 0, k == KC - 1,
                       deps=[cn_last] + (sfree_ if k == 0 else []))
                    lq = mm(bq_[:, 0:n], ONES, sq[:, l0:l0 + n], k == 0, k == KC - 1,
                            deps=[ea] + (qfree_ if k == 0 else []))
                SQC.release(sqi, [lq])
            e3 = None
            for (l0, n), (si_, bs_, sfree_, qi_, bq_, qfree_) in zip(cseg, sbanks):
                e1 = ts(MEAN[:, l0:l0 + n], bs_[:, 0:n], 1.0 / D, 0.0, ALU.mult, ALU.add, deps=[lq])
                banks.release(si_, [e1])
                e2 = stt(ACC[:, 0:n], MEAN[:, l0:l0 + n], -1.0, MEAN[:, l0:l0 + n], ALU.mult, ALU.mult, deps=[e1])
                e3 = stt(RSC[:, l0:l0 + n], bq_[:, 0:n], 1.0 / D, ACC[:, 0:n], ALU.mult, ALU.add, deps=[e2])
                banks.release(qi_, [e3])
            e4 = act(RSC[:, 0:nc_], RSC[:, 0:nc_], AF.Sqrt, deps=[e3], bias=EPSV[:, 0:1], scale=1.0)
            e5 = P.op("dve", lambda e: e.reciprocal(out=RSC[:, 0:nc_], in_=RSC[:, 0:nc_]), deps=[e4])
            for k in range(KC):
                ti, tb, tfree = TBR.get()
                e6 = tt(tb[:, 0:nc_], CN[:, k, 0:nc_], MEAN[:, 0:nc_], ALU.subtract, deps=[e5] + tfree + gb_dead)
                e7 = tt(tb[:, 0:nc_], tb[:, 0:nc_], RSC[:, 0:nc_], ALU.mult, deps=[e6])
                e8 = act(CN[:, k, 0:nc_], tb[:, 0:nc_], AF.Silu, deps=[e7],
                         scale=PV[:, PV_LCG * 16 + k:PV_LCG * 16 + k + 1],
                         bias=PV[:, PV_LCB * 16 + k:PV_LCB * 16 + k + 1])
                TBR.release(ti, [e8])
            barrier()
            SAR = Ring([view(OFF_TMPX, [512]), view(OFF_TMPX + 2048, [512])])
            SBR = Ring([view(OFF_TMPX + 4096, [512]), view(OFF_TMPX + 6144, [512])])
            TA = view(OFF_TMPX + 8192, [512])
            TB = view(OFF_TMPX + 10240, [512])
            msegs = [(0, 512, 0, p0)] + ([(512, 64, 544, 1056)] if samp else [])
            m_last = None
            for c in range(KC):
                w1i, slot1, lev1 = wload([(lambda s: s[:, 0:4096], w_pab[c])])
                w2i, slot2, lev2 = wload([(lambda s: s[:, 0:4096], w_gate[c])])
                W1 = wk(slot1, 256)
                W2 = wk(slot2, 256)
                l1 = l2 = None
                for (l0, n, hl0, xc) in msegs:
                    yai, bya, f1 = banks.get()
                    ybi, byb, f2 = banks.get()
                    gai, bga, f3 = banks.get()
                    gbi, bgb, f4 = banks.get()
                    lya = lyb = lga = lgb = None
                    for k in range(KC):
                        lya = mm(bya[:, 0:n], W1[:, k, 0:128], US[:, k, l0:l0 + n], k == 0, k == KC - 1,
                                 deps=[lev1] + (f1 if k == 0 else []))
                    for k in range(KC):
                        lyb = mm(byb[:, 0:n], W1[:, k, 128:256], CN[:, k, l0:l0 + n], k == 0, k == KC - 1,
                                 deps=(f2 if k == 0 else []))
                    for k in range(KC):
                        lga = mm(bga[:, 0:n], W2[:, k, 0:128], HM[:, k, hl0:hl0 + n], k == 0, k == KC - 1,
                                 deps=[lev2] + (f3 if k == 0 else []))
                    for k in range(KC):
                        lgb = mm(bgb[:, 0:n], W2[:, k, 128:256], HM[:, k, hl0:hl0 + n], k == 0, k == KC - 1,
                                 deps=(f4 if k == 0 else []))
                    l1, l2 = lyb, lgb
                    sai, sa, saf = SAR.get()
                    sbi, sb, sbf = SBR.get()
                    ea = act(sa[:, 0:n], bga[:, 0:n], AF.Sigmoid, deps=[lga] + saf)
                    eb = act(sb[:, 0:n], bgb[:, 0:n], AF.Sigmoid, deps=[lgb] + sbf)
                    banks.release(gai, [ea])
                    banks.release(gbi, [eb])
                    e1 = tt(TA[:, 0:n], sa[:, 0:n], bya[:, 0:n], ALU.mult, deps=[ea, lya] + ([m_last] if m_last else []))
                    e2 = tt(TB[:, 0:n], sb[:, 0:n], byb[:, 0:n], ALU.mult, deps=[eb, lyb])
                    banks.release(yai, [e1])
                    banks.release(ybi, [e2])
                    m_last = tt(MM_[:, c, l0:l0 + n], TA[:, 0:n], TB[:, 0:n], ALU.add, deps=[e1, e2])
                    SAR.release(sai, [e1])
                    SBR.release(sbi, [e2])
                wring.release(w1i, [l1])
                wring.release(w2i, [l2])
            for mp in range(8):
                wi, slot, lev = wload([(lambda s: s[:, 0:4096], w_out[mp])])
                W = wk(slot, 256)
                lm = None
                for mi in range(2):
                    m = mp * 2 + mi
                    for (l0, n, hl0, xc) in msegs:
                        bi, bk, bfree = banks.get()
                        for k in range(KC):
                            lm = mm(bk[:, 0:n], W[:, k, mi * 128:(mi + 1) * 128], MM_[:, k, l0:l0 + n],
                                    k == 0, k == KC - 1, deps=[lev, m_last] + (bfree if k == 0 else []))
                        ed = tt(X[:, m, xc:xc + n], bk[:, 0:n], X[:, m, xc:xc + n], ALU.add, deps=[lm])
                        banks.release(bi, [ed])
                wring.release(wi, [lm])
            barrier()

        if stop_after != "p0":
            ffn(PV_F1, w_gu1, w_dn1)
        if stop_after not in ("p0", "ffn1"):
            mixer_pass(0, True)
            mixer_pass(512, False)
        if stop_after not in ("p0", "ffn1", "mixer"):
            ffn(PV_F2, w_gu2, w_dn2)

        barrier()
        seg_ready = {}
        if stop_after is None:
            OSEGS = [(0, 512, 0), (512, 512, 512), (1056, 64, 1056)]
            for si_, (x0, n, l0) in enumerate(OSEGS):
                bi, bk, bfree = banks.get()
                lastm = None
                for k in range(KC):
                    si, sq, sfree = SQR.get()
                    ea = act(sq[:, 0:n], X[:, k, x0:x0 + n], AF.Square, deps=sfree)
                    lastm = mm(bk[:, 0:n], ONES, sq[:, 0:n], k == 0, k == KC - 1, deps=[ea] + (bfree if k == 0 else []))
                    SQR.release(si, [lastm])
                ea = act(RSTD[:, l0:l0 + n], bk[:, 0:n], AF.Sqrt, deps=[lastm], bias=EPSV[:, 0:1], scale=1.0 / D)
                banks.release(bi, [ea])
                er = P.op("dve", lambda e, n=n, l0=l0: e.reciprocal(out=RSTD[:, l0:l0 + n], in_=RSTD[:, l0:l0 + n]),
                          deps=[ea])
                er2 = None
                for k in range(KC):
                    er2 = stt(X[:, k, x0:x0 + n], X[:, k, x0:x0 + n],
                              PV[:, PV_FIN * 16 + k:PV_FIN * 16 + k + 1], RSTD[:, l0:l0 + n], ALU.mult, ALU.mult,
                              deps=[er])
                seg_ready[si_] = er2
        YT = Ring([view(OFF_HID, [D]), view(OFF_HID + 8192, [D]), view(OFF_HID + 16384, [D])])
        evq = 0
        for t in range(9):
            rows = 128 if t < 8 else 64
            c0 = t * 128 if t < 8 else 1056
            sdep = seg_ready.get(0 if t < 4 else (1 if t < 8 else 2))
            sdep = [sdep] if sdep else []
            i, yt, free = YT.get()
            evs = []
            for g in range(4):
                bi, bk, bfree = banks.get()
                ltr = None
                for kk in range(4):
                    k = g * 4 + kk
                    ltr = tr(bk[0:rows, kk * 128:(kk + 1) * 128], X[:, k, c0:c0 + rows], IDF,
                             deps=sdep + (bfree if kk == 0 else []))
                e = cp("act" if evq % 2 == 0 else "dve", yt[0:rows, g * 512:(g + 1) * 512], bk[0:rows, :],
                       deps=[ltr] + free)
                evq += 1
                banks.release(bi, [e])
                evs.append(e)
            od = P.dma("sp", y_o[t * 128:t * 128 + rows, :], yt[0:rows], "o_y%d" % i, deps=evs)
            YT.release(i, [od])
        for ok_ in sorted(k for k in P.dcnt if k.startswith("o_")):
            P.wait("sp", (ok_, P.dcnt[ok_]))
        P.wait("sp", ("out2", P.dcnt["out2"]))

        dkeys = sorted(P.dcnt.keys())
        import contextlib
        with contextlib.ExitStack() as stack:
            esem = {e: stack.enter_context(nc.semaphore("s_" + e)) for e in ("pe", "act", "dve", "pool")}
            dsem = {k: stack.enter_context(nc.semaphore("d_" + k)) for k in dkeys}
            block = stack.enter_context(nc.Block())
            sigval = {}
            for e in ("pe", "act", "dve", "pool"):
                need = sorted(P.needed[e])
                sigval[e] = {idx: i + 1 for i, idx in enumerate(need)}

            def emit(eng, e):
                for item in P.q[eng]:
                    if item[0] == "wait":
                        k, v = item[1], item[2]
                        if k in esem:
                            e.wait_ge(esem[k], sigval[k][v])
                        else:
                            e.wait_ge(dsem[k], v)
                    elif item[0] == "op":
                        ins = item[1](e)
                        idx = item[2]
                        if eng in sigval and idx in sigval[eng]:
                            ins.then_inc(esem[eng], 1)
                    else:
                        ins = e.dma_start(out=item[1], in_=item[2])
                        ins.then_inc(dsem[item[3]], 16)

            @block.tensor
            def _(e):
                emit("pe", e)

            @block.scalar
            def _(e):
                emit("act", e)

            @block.vector
            def _(e):
                emit("dve", e)

            @block.gpsimd
            def _(e):
                emit("pool", e)

            @block.sync
            def _(e):
                emit("sp", e)
    return nc


_CACHE = {}


def _consts():
    identf = np.eye(128, dtype=np.float32)
    jj = np.arange(128)
    maskt = (jj[:, None] <= jj[None, :]).astype(np.float32)
    p = np.arange(64)
    pj, ps_ = p // 16, p % 16
    masks = ((ps_[:, None] == ps_[None, :]) & (pj[:, None] <= pj[None, :])).astype(np.float32)
    return identf, maskt, masks


def _pack_cols(ws, tiles):
    out = np.empty((len(tiles), 128, 4096), np.float32)
    for t, ranges in enumerate(tiles):
        blk = np.concatenate([ws[a][:, c0:c0 + n] for (a, c0, n) in ranges], axis=1)
        out[t] = blk.reshape(16, 128, 256).transpose(1, 0, 2).reshape(128, 4096)
    return out


def _pack_down(w):
    out = np.empty((32, 128, 2816), np.float32)
    for g in range(4):
        for mp in range(8):
            blk = w[g * 1408:(g + 1) * 1408, mp * 256:(mp + 1) * 256]
            out[g * 8 + mp] = blk.reshape(11, 128, 256).transpose(1, 0, 2).reshape(128, 2816)
    return out


def kernel(x_prompt, x_sample, cache_conv, ffn1_norm, ffn1_w_gu, ffn1_w_down, mix_norm,
           w_in, w_s, b_s, ln_v_g, ln_v_b, w_pa, w_dw, b_dw, ln_c_g, ln_c_b, w_pb, w_out,
           ffn2_norm, ffn2_w_gu, ffn2_w_down, final_norm, _stop_after=None):
    f32 = np.float32
    A = lambda a: np.ascontiguousarray(np.asarray(a, dtype=f32))
    x_prompt, x_sample, cache_conv = A(x_prompt), A(x_sample), A(cache_conv)
    key = _stop_after
    if key not in _CACHE:
        _CACHE[key] = build_program(_stop_after)
    nc = _CACHE[key]

    identf, maskt, masks = _consts()
    vecs = [ffn1_norm[0], mix_norm[0], ffn2_norm[0], final_norm, b_dw[0], ln_c_g[0], ln_c_b[0], ln_v_g[0], ln_v_b[0]]
    pv = np.zeros((128, NPV), f32)
    for vi, v in enumerate(vecs):
        pv[:, vi * 16:(vi + 1) * 16] = A(v).reshape(16, 128).T
    wd = A(w_dw[0])
    pv[:, PV_WDW:] = wd.reshape(31, 16, 128).transpose(2, 1, 0).reshape(128, 16 * 31)
    ws = A(w_s[0])
    wst = np.ascontiguousarray(ws.transpose(2, 0, 1))
    wcol = np.ascontiguousarray(np.repeat(ws[:, 0:4, 0:4].transpose(2, 0, 1), 16, axis=0))
    bsr = A(b_s[0]).reshape(1, D)
    lnvg = A(ln_v_g[0]).reshape(1, D)
    lnvb = A(ln_v_b[0]).reshape(1, D)
    win = A(w_in[0])
    shared = {
        "pv": pv, "identf": identf, "maskt": maskt, "masks": masks, "wst": wst, "wcol": wcol,
        "bsr": bsr, "lnvg": lnvg, "lnvb": lnvb,
        "w_gu1": _pack_cols([A(ffn1_w_gu[0])], [[(0, j * 128, 128), (0, DFF + j * 128, 128)] for j in range(44)]),
        "w_dn1": _pack_down(A(ffn1_w_down[0])),
        "w_v": _pack_cols([win], [[(0, D + nb * 256, 256)] for nb in range(8)]),
        "w_u": _pack_cols([win], [[(0, cp_ * 256, 256)] for cp_ in range(8)]),
        "w_glu": _pack_cols([win], [[(0, 2 * D + c * 128, 128), (0, 3 * D + c * 128, 128)] for c in range(16)]),
        "w_gate": _pack_cols([win], [[(0, 4 * D + c * 128, 128), (0, 5 * D + c * 128, 128)] for c in range(16)]),
        "w_pab": _pack_cols([A(w_pa[0]), A(w_pb[0])], [[(0, c * 128, 128), (1, c * 128, 128)] for c in range(16)]),
        "w_out": _pack_cols([A(w_out[0])], [[(0, mp * 256, 256)] for mp in range(8)]),
        "w_gu2": _pack_cols([A(ffn2_w_gu[0])], [[(0, j * 128, 128), (0, DFF + j * 128, 128)] for j in range(44)]),
        "w_dn2": _pack_down(A(ffn2_w_down[0])),
    }
    in_maps = []
    for c in range(8):
        b, hf = c // 2, c % 2
        xc = np.zeros((NT, D), f32)
        xc[0:1024] = x_prompt[b, hf * 1024:(hf + 1) * 1024]
        if hf == 1:
            xc[1024:1056] = x_prompt[b, 992:1024]
        xs = x_sample[c * 16:(c + 1) * 16]
        xc[1056:1120] = xs.transpose(1, 0, 2).reshape(64, D)
        m = dict(shared)
        m["xin"] = xc
        m["cache"] = np.ascontiguousarray(cache_conv[0, c * 16:(c + 1) * 16])
        in_maps.append(m)

    res = run_bass_kernel_spmd(nc, in_maps, core_ids=list(range(8)))
    R = res.results

    y_prompt = np.zeros((4, 2048, D), f32)
    y_sample = np.zeros((128, 4, D), f32)
    cvp = np.zeros((1, 4, 128, D), f32)
    cvs = np.zeros((1, 128, 4, D), f32)
    csp = np.zeros((1, 4, 30, D), f32)
    css = np.zeros((1, 128, 30, D), f32)
    for c in range(8):
        b, hf = c // 2, c % 2
        r = R[c]
        y_prompt[b, hf * 1024:(hf + 1) * 1024] = r["y"][0:1024]
        y_sample[c * 16:(c + 1) * 16] = r["y"][1024:1088].reshape(4, 16, D).transpose(1, 0, 2)
        cvs[0, c * 16:(c + 1) * 16] = r["vsamp"].reshape(4, 16, D).transpose(1, 0, 2)
        css[0, c * 16:(c + 1) * 16, 0:26] = r["csso"]
        css[0, c * 16:(c + 1) * 16, 26:30] = r["cssn"].reshape(4, 16, D).transpose(1, 0, 2)
        if hf == 1:
            cvp[0, b] = r["vlast"]
            csp[0, b] = r["csp"]
    return (y_prompt, y_sample, cvp, cvs, csp, css)
```
